# Optimizing a Trainium2 kernel written in Bass

```python
import jax
import jax.numpy as jnp
from jax import lax
import numpy as np

D_MODEL = 2048
BATCH = 4
SEQ = 2048
DEPTH = 1

N_HEADS = 16
HEAD_DIM = 128
N_KV_GROUPS = 4
HEADS_PER_GROUP = N_HEADS // N_KV_GROUPS
ATTN_WIDTH = N_HEADS * HEAD_DIM
KV_WIDTH = N_KV_GROUPS * HEAD_DIM
CMP_BLOCK = 32
CMP_STRIDE = 16
CMP_HIDDEN = 256
SEL_BLOCK = 64
SEL_TOP_N = 8
SEL_QUERY_CHUNK = 64
WINDOW = 512
WIN_QBLOCK = 128
SCALE = HEAD_DIM ** -0.5
CONV_WIDTH = D_MODEL
CONV_K = 3
N_BRANCHES = 2
RMS_EPS = 1e-6
NEG_INF = -1e30
FORCED_SCORE = 1e4

IN_SPLIT_SIZES = (ATTN_WIDTH,
                  KV_WIDTH, KV_WIDTH,
                  KV_WIDTH, KV_WIDTH,
                  KV_WIDTH, KV_WIDTH,
                  3 * N_HEADS,
                  ATTN_WIDTH,
                  CONV_WIDTH, CONV_WIDTH, CONV_WIDTH,
                  CONV_WIDTH,
                  N_BRANCHES * D_MODEL)
N_IN = sum(IN_SPLIT_SIZES)

kernel_name = "nsa_shortconv_gated_hybrid"


def rmsnorm(x, g):
    xf = x.astype(jnp.float32)
    xf = xf * lax.rsqrt(jnp.mean(xf * xf, axis=-1, keepdims=True) + RMS_EPS)
    return xf.astype(x.dtype) * g


def masked_softmax(s, mask):
    s = jnp.where(mask, s.astype(jnp.float32), NEG_INF)
    p = jax.nn.softmax(s, axis=-1)
    return jnp.where(mask, p, 0.0)


def compress_blocks(kv, pe, w1, w2):
    B, S, G, Dh = kv.shape
    n_cmp = (S - CMP_BLOCK) // CMP_STRIDE + 1
    idx = np.arange(n_cmp)[:, None] * CMP_STRIDE + np.arange(CMP_BLOCK)[None, :]
    blocks = kv[:, idx] + pe[None, None, :, None, :]
    flat = blocks.transpose(0, 1, 3, 2, 4).reshape(B, n_cmp, G, CMP_BLOCK * Dh)
    return jax.nn.silu(flat @ w1) @ w2


def compressed_attention(q, kcb, vcb):
    S = q.shape[1]
    n_cmp = kcb.shape[1]
    t = np.arange(S)
    blk_end = np.arange(n_cmp) * CMP_STRIDE + CMP_BLOCK - 1
    mask = jnp.asarray(blk_end[None, :] <= t[:, None])
    s = jnp.einsum('bsghd,bngd->bghsn', q, kcb) * SCALE
    p = masked_softmax(s, mask)
    o = jnp.einsum('bghsn,bngd->bsghd', p.astype(vcb.dtype), vcb)
    return o, p


def select_blocks(p_cmp, S):
    n_cmp = p_cmp.shape[-1]
    n_slc = S // SEL_BLOCK
    c0 = np.arange(n_cmp)[:, None] * CMP_STRIDE
    s0 = np.arange(n_slc)[None, :] * SEL_BLOCK
    overlap = np.maximum(0, np.minimum(c0 + CMP_BLOCK, s0 + SEL_BLOCK) - np.maximum(c0, s0))
    agg = jnp.asarray(overlap / CMP_BLOCK, dtype=jnp.float32)
    imp = jnp.einsum('bghsn,nj->bgsj', p_cmp, agg)
    t = np.arange(S)[:, None]
    j = np.arange(n_slc)[None, :]
    cur = t // SEL_BLOCK
    causal = j * SEL_BLOCK <= t
    forced = ((j == 0) | (j == cur) | (j == cur - 1)) & causal
    imp = jnp.where(jnp.asarray(forced), FORCED_SCORE, imp)
    imp = jnp.where(jnp.asarray(causal), imp, -1.0)
    _, idx = lax.top_k(imp, min(SEL_TOP_N, n_slc))
    return idx


def selected_attention(q, k, v, idx):
    B, S, G, h, Dh = q.shape
    n_slc = S // SEL_BLOCK
    n_top = idx.shape[-1]
    nc = S // SEL_QUERY_CHUNK
    kb = k.reshape(B, n_slc, SEL_BLOCK, G, Dh).transpose(0, 3, 1, 2, 4)
    vb = v.reshape(B, n_slc, SEL_BLOCK, G, Dh).transpose(0, 3, 1, 2, 4)
    q_ch = q.reshape(B, nc, SEL_QUERY_CHUNK, G, h, Dh).transpose(1, 0, 2, 3, 4, 5)
    i_ch = idx.reshape(B, G, nc, SEL_QUERY_CHUNK, n_top).transpose(2, 0, 1, 3, 4)
    t_ch = jnp.arange(S, dtype=jnp.int32).reshape(nc, SEL_QUERY_CHUNK)
    bi = jnp.arange(B)[:, None, None, None]
    gi = jnp.arange(G)[None, :, None, None]
    offs = jnp.arange(SEL_BLOCK, dtype=jnp.int32)

    def chunk(args):
        qc, ic, tc = args
        kg = kb[bi, gi, ic]
        vg = vb[bi, gi, ic]
        s = jnp.einsum('bqghd,bgqnkd->bgqhnk', qc, kg) * SCALE
        kpos = ic[..., None] * SEL_BLOCK + offs
        mask = kpos <= tc[None, None, :, None, None]
        qn = qc.shape[1]
        s = s.reshape(B, G, qn, h, n_top * SEL_BLOCK)
        mask = mask.reshape(B, G, qn, 1, n_top * SEL_BLOCK)
        p = masked_softmax(s, mask).astype(vg.dtype)
        return jnp.einsum('bgqhm,bgqmd->bqghd', p, vg.reshape(B, G, qn, n_top * SEL_BLOCK, Dh))

    o = lax.map(chunk, (q_ch, i_ch, t_ch))
    return o.transpose(1, 0, 2, 3, 4, 5).reshape(B, S, G, h, Dh)


def window_attention(q, k, v):
    B, S, G, h, Dh = q.shape
    nqb = S // WIN_QBLOCK
    n_off = WINDOW // WIN_QBLOCK + 1
    kpad = jnp.pad(k, ((0, 0), (WINDOW, 0), (0, 0), (0, 0)))
    vpad = jnp.pad(v, ((0, 0), (WINDOW, 0), (0, 0), (0, 0)))
    k_band = jnp.concatenate([kpad[:, j * WIN_QBLOCK: j * WIN_QBLOCK + S].reshape(B, nqb, WIN_QBLOCK, G, Dh)
                              for j in range(n_off)], axis=2)
    v_band = jnp.concatenate([vpad[:, j * WIN_QBLOCK: j * WIN_QBLOCK + S].reshape(B, nqb, WIN_QBLOCK, G, Dh)
                              for j in range(n_off)], axis=2)
    kb_len = n_off * WIN_QBLOCK
    qpos = np.arange(nqb)[:, None, None] * WIN_QBLOCK + np.arange(WIN_QBLOCK)[None, :, None]
    kpos = np.arange(nqb)[:, None, None] * WIN_QBLOCK + np.arange(kb_len)[None, None, :] - WINDOW
    mask = jnp.asarray((kpos >= 0) & (kpos <= qpos) & (kpos > qpos - WINDOW))
    qb = q.reshape(B, nqb, WIN_QBLOCK, G, h, Dh)
    s = jnp.einsum('bnqghd,bnkgd->bnghqk', qb, k_band) * SCALE
    p = masked_softmax(s, mask[None, :, None, None]).astype(v.dtype)
    o = jnp.einsum('bnghqk,bnkgd->bnqghd', p, v_band)
    return o.reshape(B, S, G, h, Dh)


def short_conv(u, c_b, c_c, conv_w, conv_b):
    v = c_c * u
    y = lax.conv_general_dilated(v, conv_w[:, None, :].astype(v.dtype), window_strides=(1,),
                                 padding=[(CONV_K - 1, 0)],
                                 dimension_numbers=('NWC', 'WIO', 'NWC'),
                                 feature_group_count=CONV_WIDTH)
    return c_b * (y + conv_b)


def hybrid_layer(x, norm_g, w_in, b_in, pe_k, w1_k, w2_k, pe_v, w1_v, w2_v,
                 conv_w, conv_b, p_attn, p_conv, w_o):
    B, S, D = x.shape
    G, h, Dh = N_KV_GROUPS, HEADS_PER_GROUP, HEAD_DIM
    hn = rmsnorm(x, norm_g)
    proj = hn @ w_in + b_in
    cuts = np.cumsum(IN_SPLIT_SIZES)[:-1].tolist()
    (q, kc, vc, ks, vs, kw, vw, g_nsa, z_attn,
     u, c_c, c_b, z_conv, g_merge) = jnp.split(proj, cuts, axis=-1)
    q = q.reshape(B, S, G, h, Dh)
    kv = lambda a: a.reshape(B, S, G, Dh)
    kcb = compress_blocks(kv(kc), pe_k, w1_k, w2_k)
    vcb = compress_blocks(kv(vc), pe_v, w1_v, w2_v)
    o_cmp, p_cmp = compressed_attention(q, kcb, vcb)
    idx = select_blocks(p_cmp, S)
    o_sel = selected_attention(q, kv(ks), kv(vs), idx)
    o_win = window_attention(q, kv(kw), kv(vw))
    gb = jax.nn.sigmoid(g_nsa).reshape(B, S, 3, G, h, 1)
    o_attn = (gb[:, :, 0] * o_cmp + gb[:, :, 1] * o_sel + gb[:, :, 2] * o_win).reshape(B, S, ATTN_WIDTH)
    y_attn = (o_attn * jax.nn.silu(z_attn)) @ p_attn
    y_conv = (short_conv(u, c_b, c_c, conv_w, conv_b) * jax.nn.silu(z_conv)) @ p_conv
    gm = jax.nn.sigmoid(g_merge).reshape(B, S, N_BRANCHES, D)
    y = gm[:, :, 0] * y_attn + gm[:, :, 1] * y_conv
    return x + y @ w_o


def setup_inputs(seed: int = 0) -> dict:
    key = jax.random.key(seed)
    ks = jax.random.split(key, 17)
    f32 = jnp.float32
    L = DEPTH
    nrm = lambda k, shape, fan: jax.random.normal(k, shape, f32) * (fan ** -0.5)
    return {
        "x": jax.random.normal(ks[0], (BATCH, SEQ, D_MODEL), f32),
        "norm_g": 1.0 + 0.02 * jax.random.normal(ks[1], (L, D_MODEL), f32),
        "w_in": nrm(ks[2], (L, D_MODEL, N_IN), D_MODEL),
        "b_in": 0.02 * jax.random.normal(ks[3], (L, N_IN), f32),
        "cmp_pe_k": 0.1 * jax.random.normal(ks[4], (L, CMP_BLOCK, HEAD_DIM), f32),
        "cmp_w1_k": nrm(ks[5], (L, CMP_BLOCK * HEAD_DIM, CMP_HIDDEN), CMP_BLOCK * HEAD_DIM),
        "cmp_w2_k": nrm(ks[6], (L, CMP_HIDDEN, HEAD_DIM), CMP_HIDDEN),
        "cmp_pe_v": 0.1 * jax.random.normal(ks[7], (L, CMP_BLOCK, HEAD_DIM), f32),
        "cmp_w1_v": nrm(ks[8], (L, CMP_BLOCK * HEAD_DIM, CMP_HIDDEN), CMP_BLOCK * HEAD_DIM),
        "cmp_w2_v": nrm(ks[9], (L, CMP_HIDDEN, HEAD_DIM), CMP_HIDDEN),
        "conv_w": nrm(ks[10], (L, CONV_K, CONV_WIDTH), CONV_K),
        "conv_b": 0.02 * jax.random.normal(ks[11], (L, CONV_WIDTH), f32),
        "p_attn": nrm(ks[12], (L, ATTN_WIDTH, D_MODEL), ATTN_WIDTH),
        "p_conv": nrm(ks[13], (L, CONV_WIDTH, D_MODEL), CONV_WIDTH),
        "w_o": nrm(ks[14], (L, D_MODEL, D_MODEL), D_MODEL),
        "final_g": 1.0 + 0.02 * jax.random.normal(ks[15], (D_MODEL,), f32),
    }


def reference(x, norm_g, w_in, b_in, cmp_pe_k, cmp_w1_k, cmp_w2_k, cmp_pe_v, cmp_w1_v,
              cmp_w2_v, conv_w, conv_b, p_attn, p_conv, w_o, final_g):
    for l in range(DEPTH):
        x = hybrid_layer(x, norm_g[l], w_in[l], b_in[l], cmp_pe_k[l], cmp_w1_k[l], cmp_w2_k[l],
                         cmp_pe_v[l], cmp_w1_v[l], cmp_w2_v[l], conv_w[l], conv_b[l],
                         p_attn[l], p_conv[l], w_o[l])
    return rmsnorm(x, final_g)
```

```python
import contextlib
import numpy as np
import concourse.bass as bass
import concourse.mybir as mybir
from concourse.bass_utils import run_bass_kernel_spmd

F32 = mybir.dt.float32
BF16 = mybir.dt.bfloat16
AF = mybir.ActivationFunctionType
ALU = mybir.AluOpType
AX = mybir.AxisListType

D = 2048
S = 2048
OWN = 1024
N_IN = 19504
SCALE = 128 ** -0.5
NEG = -30000.0
EPS = 1e-6
OFF = dict(q=0, kc=2048, vc=2560, ks=3072, vs=3584, kw=4096, vw=4608, gn=5120, za=5168,
           u=7216, cc=9264, cb=11312, zc=13360, gm0=15408, gm1=17456)
FMSEG = [("q", 16), ("kc", 4), ("vc", 4), ("ks", 4), ("kw", 4), ("za", 16), ("u", 16), ("cc", 16),
         ("cb", 16), ("zc", 16), ("gm0", 16), ("gm1", 16)]
FMB = {}
_t = 0
for _n, _c in FMSEG:
    FMB[_n] = _t
    _t += _c
NFM = _t
CA_NG = NFM
CA_CW = CA_NG + 16
CA_CB = CA_CW + 48
CA_KV = CA_CB + 16
CA_CV = CA_KV + 16
CA_HV = CA_CV + 1
CA_PEK = CA_HV + 1
CA_PEV = CA_PEK + 32
CA_BGN = CA_PEV + 32
CA_N = CA_BGN + 48
CB_ID = 0
CB_TLO = 128
CB_THI = 256
CB_E = 384
CB_CMP = CB_E + 2048
CB_AGG = CB_CMP + 1024
CB_N = CB_AGG + 32

ENGS = ("pe", "act", "dve", "pool", "sp")


class Prog:
    def __init__(self, nc):
        self.nc = nc
        self.ops = {e: [] for e in ENGS}
        self.last_w = {}
        self.readers = {}
        self.clock = {e: {} for e in ENGS}
        self.dma_slots = {}

    def _need(self, eng, tok, waits):
        src, idx = tok
        if src == eng and eng == "pe":
            return
        if self.clock[eng].get(src, 0) >= idx:
            return
        waits[src] = max(waits.get(src, 0), idx)

    def op(self, eng, fn, reads=(), writes=(), dma_slot=None, ndma=1):
        waits = {}
        for k in reads:
            t = self.last_w.get(k)
            if t is not None:
                self._need(eng, t, waits)
        for k in writes:
            t = self.last_w.get(k)
            if t is not None:
                self._need(eng, t, waits)
            for t in self.readers.get(k, ()):
                self._need(eng, t, waits)
        lst = self.ops[eng]
        idx = len(lst) + 1
        if dma_slot is not None:
            self.dma_slots[dma_slot] = self.dma_slots.get(dma_slot, 0) + ndma
            tok = (("dma", dma_slot), self.dma_slots[dma_slot])
        else:
            tok = (eng, idx)
        ck = self.clock[eng]
        for src, v in waits.items():
            ck[src] = max(ck.get(src, 0), v)
            if isinstance(src, str):
                pc = self.ops[src][v - 1]["clock"]
                for s2, v2 in pc.items():
                    if s2 != eng:
                        ck[s2] = max(ck.get(s2, 0), v2)
        lst.append(dict(fn=fn, waits=waits, tok=tok, clock=dict(ck), dma_slot=dma_slot, ndma=ndma))
        for k in reads:
            self.readers.setdefault(k, []).append(tok)
        for k in writes:
            self.last_w[k] = tok
            self.readers[k] = []
        return tok

    def barrier(self):
        toks = []
        for e in ENGS:
            for i in range(len(self.ops[e]) - 1, -1, -1):
                r = self.ops[e][i]
                if r["fn"] is not None and r["dma_slot"] is None:
                    toks.append((e, i + 1))
                    break
        for s, c in self.dma_slots.items():
            toks.append((("dma", s), c))
        for e in ENGS:
            waits = {}
            for t in toks:
                if t[0] == e and e == "pe":
                    continue
                self._need(e, t, waits)
            ck = self.clock[e]
            for src, v in waits.items():
                ck[src] = max(ck.get(src, 0), v)
            self.ops[e].append(dict(fn=None, waits=waits, tok=(e, len(self.ops[e]) + 1),
                                    clock=dict(ck), dma_slot=None, ndma=0))

    def emit(self, final_wait_slots=()):
        nc = self.nc
        need = {e: set() for e in ENGS}
        for e in ENGS:
            for r in self.ops[e]:
                for src, v in r["waits"].items():
                    if isinstance(src, str):
                        need[src].add(v)
        rank = {}
        for e in ENGS:
            for i, v in enumerate(sorted(need[e])):
                rank[(e, v)] = i + 1
        with contextlib.ExitStack() as st:
            esem = {e: st.enter_context(nc.semaphore("s_" + e)) for e in ENGS}
            dsem = {}
            for i, s in enumerate(self.dma_slots):
                dsem[s] = st.enter_context(nc.semaphore("d_%d" % i))
            block = st.enter_context(nc.Block())

            def run(e, engobj):
                for i, r in enumerate(self.ops[e]):
                    for src, v in r["waits"].items():
                        if isinstance(src, str):
                            engobj.wait_ge(esem[src], rank[(src, v)])
                        else:
                            engobj.wait_ge(dsem[src[1]], 16 * v)
                    if r["fn"] is None:
                        assert (e, i + 1) not in rank
                        continue
                    ins = r["fn"](engobj)
                    if r["dma_slot"] is not None:
                        insl = ins if isinstance(ins, (list, tuple)) else [ins]
                        assert len(insl) == r["ndma"]
                        for x in insl:
                            x.then_inc(dsem[r["dma_slot"]], 16)
                        assert (e, i + 1) not in rank
                    elif (e, i + 1) in rank:
                        ins.then_inc(esem[e], 1)
                if e == "sp":
                    for s in final_wait_slots:
                        engobj.wait_ge(dsem[s], 16 * self.dma_slots[s])

            @block.tensor
            def _(eng):
                run("pe", eng)

            @block.scalar
            def _(eng):
                run("act", eng)

            @block.vector
            def _(eng):
                run("dve", eng)

            @block.gpsimd
            def _(eng):
                run("pool", eng)

            @block.sync
            def _(eng):
                run("sp", eng)


def build_program(dbg=None):
    nc = bass.Bass("TRN2", target_bir_lowering=False)
    dt_in = lambda n, s: nc.dram_tensor(n, s, F32, kind="ExternalInput").ap()
    xctx = dt_in("xctx", [S, D])
    w_in = dt_in("w_in", [D, N_IN])
    p_attn = dt_in("p_attn", [D, D])
    p_conv = dt_in("p_conv", [D, D])
    w_o = dt_in("w_o", [D, D])
    w1k = dt_in("w1k", [4096, 256])
    w1v = dt_in("w1v", [4096, 256])
    w2d = dt_in("w2", [128, 4 * 128])
    cAd = dt_in("cA", [128, CA_N])
    cBd = dt_in("cB", [128, CB_N])
    cCd = dt_in("cC", [128, 768])
    bvd = dt_in("bv", [128, 1024])
    fgd = dt_in("fg", [128, D])
    outd = nc.dram_tensor("out", [OWN, D], F32, kind="ExternalOutput").ap()

    with contextlib.ExitStack() as st:
        T = lambda name, shape, dt: st.enter_context(nc.sbuf_tensor("sb_" + name, shape, dt))
        hnT = T("hnT", [128, 16, S + 2], BF16)
        R3 = T("R3", [128, 16384], BF16)
        KV = T("KV", [128, 28784], BF16)
        pan = [T("pan%d" % i, [128, 16, 256], BF16) for i in range(2)]
        expT = T("expT", [128, 4, OWN], BF16)
        cA = T("cA", [128, CA_N], F32)
        cB = T("cB", [128, CB_N], BF16)
        cC = T("cC", [128, 768], F32)
        w2 = T("w2", [128, 4, 128], BF16)
        peT = T("peT", [128, 64], BF16)
        kcbT = T("kcbT", [128, 4, 127], BF16)
        vcb = T("vcb", [128, 4, 161], BF16)
        hid = T("hid", [128, 4, 2, 127], BF16)
        hb = T("hb", [128, 2], F32)
        gsig = T("gsig", [128, 8, 48], F32)
        gtmp = T("gtmp", [128, 48], F32)
        gnw = T("gnw", [128, 16, 48], BF16)
        stat = T("stat", [128, 64], F32)
        pT = [T("pT%d" % i, [128, 512], BF16) for i in range(3)]
        ocomb = [T("ocomb%d" % i, [128, 512], F32) for i in range(2)]
        obf = T("obf", [128, 512], BF16)
        sm = T("sm", [128, 2, 3, 12], F32)
        impw = T("impw", [128, 4, 32], F32)
        imp = T("imp", [128, 4, 32], F32)
        top8 = T("top8", [128, 8], F32)
        selb = T("selb", [128, 128], BF16)
        selbT = T("selbT", [128, 2, 128], BF16)
        psb = [st.enter_context(nc.psum_tensor("psum%d" % i, [128, 512], F32)) for i in range(8)]

        pg = Prog(nc)
        op = pg.op

        oT = hnT[:, :, 0:OWN]
        resid = hnT[:].bitcast(F32).rearrange("p (a two) c -> p a (two c)", two=2)[:, :, 0:2048]
        R3f = R3[:].bitcast(F32)
        xb = [R3f[:, 0:2048], R3f[:, 2048:4096], R3f[:, 4096:6144]]
        xn = [R3[:, 12288:14336], R3[:, 14336:16384]]
        qT = R3[:].rearrange("p (h t) -> p h t", h=16)
        ycT = qT
        fgb = R3f[:, 0:2048]
        kTb = KV[:, 0:8192].rearrange("p (g t) -> p g t", g=4)
        kwT = KV[:, 8192:14336].rearrange("p (g t) -> p g t", g=4)
        vsA = KV[:, 14336:22592].rearrange("p (t g c) -> p t g c", t=16, g=4)
        vwA = KV[:, 22592:28784].rearrange("p (t g c) -> p t g c", t=12, g=4)
        yT = KV[:, 0:16384].rearrange("p (j t) -> p j t", j=16)
        wk = KV[:, 16384:28784].bitcast(F32)
        ef = expT[:].rearrange("p a b -> p (a b)").bitcast(F32)
        bvs = ef[:, 0:512]
        bvw = ef[:, 512:1024]
        ident = cB[:, CB_ID:CB_ID + 128]
        tri_lo = cB[:, CB_TLO:CB_TLO + 128]
        tri_hi = cB[:, CB_THI:CB_THI + 128]
        Emat = cB[:, CB_E:CB_E + 2048].rearrange("p (k c) -> p k c", k=16)
        cmpb = cB[:, CB_CMP:CB_CMP + 1024]
        F1e4 = cC[:, 0:256].rearrange("p (q j) -> p q j", q=8)
        cvb = cC[:, 256:512].rearrange("p (q j) -> p q j", q=8)
        cvbm1 = cC[:, 512:768].rearrange("p (q j) -> p q j", q=8)

        def psbf(b):
            return psb[b][:].bitcast(BF16)

        def load_consts():
            op("sp", lambda e: e.dma_start(out=cA[:], in_=cAd), writes=["cA"], dma_slot="cA")
            op("pool", lambda e: e.dma_start(out=cB[:], in_=cBd), writes=["cB"], dma_slot="cB")
            op("sp", lambda e: e.dma_start(out=cC[:], in_=cCd), writes=["cC"], dma_slot="cC")
            op("sp", lambda e: e.dma_start(out=ef[:, 0:1024], in_=bvd), writes=["bv"], dma_slot="bv")
            op("pool", lambda e: e.dma_start(out=w2[:].rearrange("p a b -> p (a b)"), in_=w2d), writes=["w2"],
               dma_slot="w2")
            op("pool", lambda e: e.dma_start(out=peT[:], in_=cAd[:, CA_PEK:CA_PEK + 64]), writes=["peT"],
               dma_slot="peT")
            gv = w_in[:, OFF["gn"]:OFF["gn"] + 48].rearrange("(kc p) n -> p kc n", p=128)
            op("pool", lambda e: [e.dma_start(out=gnw[:, 0:8, :], in_=gv[:, 0:8, :]),
                                  e.dma_start(out=gnw[:, 8:16, :], in_=gv[:, 8:16, :])],
               writes=["gnw"], dma_slot="gnw", ndma=2)
        def init_small():
            op("dve", lambda e: e.memset(vsA[:, :, :, 128:129], 1.0), writes=["vs1"])
            op("dve", lambda e: e.memset(vwA[:, :, :, 128:129], 1.0), writes=["vw1"])
            op("dve", lambda e: e.memset(vcb[:, :, 128:129], 1.0), writes=["vcb1"])
            op("dve", lambda e: e.memset(selb[:], 0.0), writes=["selb"])
            for g in range(4):
                op("dve", lambda e, g=g: e.tensor_copy(out=vcb[:, g, 129:161], in_=cB[:, CB_AGG:CB_AGG + 32]),
                   reads=["cB"], writes=["vcb1"])

        jobs = []

        def run_jobs():
            pj = [i for i, j in enumerate(jobs) if j[0] is not None]
            loaded = 0
            slot_of = {}

            def load(k):
                i = pj[k]
                src, ncols, _ = jobs[i]
                s = k % 2
                slot_of[i] = s
                v = src.rearrange("(kc p) n -> p kc n", p=128)
                op("pool", lambda e: [e.dma_start(out=pan[s][:, 0:8, 0:ncols], in_=v[:, 0:8, :]),
                                      e.dma_start(out=pan[s][:, 8:16, 0:ncols], in_=v[:, 8:16, :])],
                   writes=[("pan", s)], dma_slot=("pan", s), ndma=2)

            for i, (src, ncols, fn) in enumerate(jobs):
                if src is not None:
                    k = pj.index(i)
                    while loaded < min(len(pj), k + 2):
                        load(loaded)
                        loaded += 1
                    fn(slot_of[i])
                else:
                    fn(None)

        bank_ctr = [0]

        def next_bank(nb=8):
            b = bank_ctr[0] % nb
            bank_ctr[0] += 1
            return b

        def hn_keys(t0, n):
            return [("hn", t) for t in range(t0 // 128, (t0 + n + 127) // 128)]

        def mm(out, lhsT, rhs, start, stop, reads, writes):
            op("pe", lambda e: e.matmul(out, lhsT=lhsT, rhs=rhs, start=start, stop=stop, skip_group_check=True),
               reads=reads, writes=writes)

        def fm_tile(slot, coff, chunks, evac, src=None, src_keys=None, nb=8):
            for ci, (t0, n) in enumerate(chunks):
                b = next_bank(nb)
                ps = psb[b][:, 0:n]
                for kc in range(16):
                    if src is None:
                        rhs = hnT[:, kc, t0:t0 + n]
                        rk = hn_keys(t0, n)
                    else:
                        rhs = src[:, kc, t0:t0 + n]
                        rk = src_keys
                    mm(ps, pan[slot][:, kc, coff:coff + 128], rhs, kc == 0, kc == 15,
                       [("pan", slot)] + rk, [("ps", b)])
                evac(ci, b, ps, t0, n)

        def bias_col(seg, t):
            return cA[:, FMB[seg] + t:FMB[seg] + t + 1]

        def stage1(tt):
            x_ = xb[tt % 3]
            xn_ = xn[tt % 2]
            xk = ("xb", tt % 3)
            op("sp", lambda e, tt=tt, x_=x_: e.dma_start(out=x_, in_=xctx[tt * 128:(tt + 1) * 128, :]),
               writes=[xk], dma_slot=xk)
            op("act", lambda e, tt=tt, x_=x_, xn_=xn_: e.activation(out=xn_, in_=x_, func=AF.Square,
                                                                   accum_out=stat[:, tt:tt + 1]),
               reads=[xk], writes=[("xn", tt % 2), ("ss", tt)])
            op("act", lambda e, tt=tt: e.activation(out=stat[:, 16 + tt:17 + tt], in_=stat[:, tt:tt + 1],
                                                    func=AF.Sqrt, scale=1.0 / D, bias=EPS),
               reads=[("ss", tt)], writes=[("rms", tt)])
            op("dve", lambda e, tt=tt: e.reciprocal(out=stat[:, 32 + tt:33 + tt], in_=stat[:, 16 + tt:17 + tt]),
               reads=[("rms", tt)], writes=[("rstd", tt)])
            op("dve", lambda e, tt=tt, x_=x_, xn_=xn_: e.tensor_scalar(
                out=xn_, in0=x_, scalar1=stat[:, 32 + tt:33 + tt], scalar2=None, op0=ALU.mult),
               reads=[xk, ("rstd", tt)], writes=[("xn", tt % 2)])

        def stage2(tt):
            xn_ = xn[tt % 2]
            for half in range(2):
                b = next_bank()
                pv = psbf(b)
                for k8 in range(8):
                    kc = half * 8 + k8
                    op("pe", lambda e, pv=pv, k8=k8, kc=kc, xn_=xn_: e.transpose(
                        out=pv[:, k8 * 128:(k8 + 1) * 128], in_=xn_[:, kc * 128:(kc + 1) * 128], identity=ident),
                       reads=[("xn", tt % 2), "cB"], writes=[("ps", b)])
                op("dve", lambda e, pv=pv, half=half, tt=tt: e.tensor_tensor(
                    out=hnT[:, half * 8:half * 8 + 8, tt * 128:(tt + 1) * 128],
                    in0=pv.rearrange("p (k t) -> p k t", k=8),
                    in1=cA[:, CA_NG + half * 8:CA_NG + half * 8 + 8].unsqueeze(2).to_broadcast([128, 8, 128]),
                    op=ALU.mult),
                   reads=[("ps", b), "cA"], writes=[("hn", tt)])

        stage1(0)
        load_consts()
        init_small()
        for tt in range(16):
            if tt + 1 < 16:
                stage1(tt + 1)
            stage2(tt)
        op("dve", lambda e: e.tensor_copy(out=hnT[:, :, 2048:2050], in_=hnT[:, :, 1022:1024]), reads=[("hn", 7)], writes=["halo"])

        ALLCH = [(0, 512), (512, 512), (1024, 512), (1536, 512)]
        OWNCH = [(1024, 512), (1536, 512)]

        def kT_job(seg, pi, dst, chunks, tshift=0, key=None, extra_w=()):
            key = key or (seg + "T")

            def fn(slot):
                for ct in range(2):
                    t = pi * 2 + ct

                    def evac(ci, b, ps, t0, n, t=t):
                        op("act", lambda e: e.activation(out=dst[:, t, t0 - tshift:t0 - tshift + n], in_=ps,
                                                         func=AF.Identity, bias=bias_col(seg, t)),
                           reads=[("ps", b), "cA"], writes=[(key, t)] + list(extra_w))
                    fm_tile(slot, ct * 128, chunks, evac)
            jobs.append((w_in[:, OFF[seg] + pi * 256:OFF[seg] + pi * 256 + 256], 256, fn))

        def compress_jobs(kind, w1d):
            src_key = [("kTb", g) for g in range(4)]
            pe_off = 0 if kind == "k" else 32

            def region(g, hc):
                r = g * 2 + hc
                return 4 + r // 4, (r % 4) * 127

            def fn_factory(half):
                def fn(slot):
                    for il in range(16):
                        i = half * 16 + il
                        for hc in range(2):
                            mm(psb[6][:, hc:hc + 1], pan[slot][:, il, hc * 128:(hc + 1) * 128],
                               peT[:, pe_off + i:pe_off + i + 1], (i == 0 and hc == 0), i == 31,
                               [("pan", slot), "peT"], [("ps", 6)])
                        for g in range(4):
                            for hc in range(2):
                                b, c = region(g, hc)
                                mm(psb[b][:, c:c + 127], pan[slot][:, il, hc * 128:(hc + 1) * 128],
                                   kTb[:, g, i:i + 16 * 126 + 1:16], (i == 0 and c == 0), i == 31,
                                   [("pan", slot)] + src_key, [("ps", b)])
                    if half == 1:
                        op("dve", lambda e: e.tensor_copy(out=hb[:], in_=psb[6][:, 0:2]), reads=[("ps", 6)], writes=["hb"])
                        for g in range(4):
                            for hc in range(2):
                                b, c = region(g, hc)
                                op("act", lambda e, g=g, hc=hc, b=b, c=c: e.activation(
                                    out=hid[:, g, hc, :], in_=psb[b][:, c:c + 127], func=AF.Silu, bias=hb[:, hc:hc + 1]),
                                   reads=[("ps", b), "hb"], writes=[("hid", g, hc)])
                        wb = 0 if kind == "k" else 2
                        for g in range(4):
                            for hc in range(2):
                                if kind == "k":
                                    mm(psb[g][:, 0:127], w2[:, wb + hc, :], hid[:, g, hc, :], hc == 0, hc == 1,
                                       ["w2", ("hid", g, hc)], [("ps", g)])
                                else:
                                    mm(psb[g][0:127, 0:128], hid[:, g, hc, :], w2[:, wb + hc, :], hc == 0, hc == 1,
                                       ["w2", ("hid", g, hc)], [("ps", g)])
                            if kind == "k":
                                op("dve", lambda e, g=g: e.tensor_copy(out=kcbT[:, g, :], in_=psb[g][:, 0:127]),
                                   reads=[("ps", g)], writes=[("kcbT", g)])
                            else:
                                op("dve", lambda e, g=g: e.tensor_copy(out=vcb[0:127, g, 0:128], in_=psb[g][0:127, 0:128]),
                                   reads=[("ps", g)], writes=[("vcb", g)])
                return fn
            jobs.append((w1d[0:2048, :], 256, fn_factory(0)))
            jobs.append((w1d[2048:4096, :], 256, fn_factory(1)))

        for pi in range(2):
            kT_job("kc", pi, kTb, ALLCH, key="kTb")
        compress_jobs("k", w1k)
        for pi in range(2):
            kT_job("vc", pi, kTb, ALLCH, key="kTb")
        compress_jobs("v", w1v)
        for pi in range(2):
            kT_job("ks", pi, kTb, ALLCH, key="kTb")
        for pi in range(2):
            kT_job("kw", pi, kwT, ALLCH[1:], tshift=512)

        def v_job(seg, pi, dstA, tiles, tile_shift, bv):
            def fn(slot):
                for tt in tiles:
                    b = next_bank()
                    ps = psb[b][:, 0:256]
                    for kc in range(16):
                        mm(ps, hnT[:, kc, tt * 128:(tt + 1) * 128], pan[slot][:, kc, 0:256], kc == 0, kc == 15,
                           [("pan", slot), ("hn", tt)], [("ps", b)])
                    op("dve", lambda e, tt=tt, ps=ps: e.tensor_tensor(
                        out=dstA[:, tt - tile_shift, pi * 2:pi * 2 + 2, 0:128],
                        in0=ps.rearrange("p (g d) -> p g d", g=2),
                        in1=bv[:, pi * 256:(pi + 1) * 256].rearrange("p (g d) -> p g d", g=2), op=ALU.add),
                       reads=[("ps", b), "bv"], writes=[(seg, tt, pi)])
            jobs.append((w_in[:, OFF[seg] + pi * 256:OFF[seg] + pi * 256 + 256], 256, fn))

        for pi in range(2):
            v_job("vs", pi, vsA, list(range(16)), 0, bvs)
        for pi in range(2):
            v_job("vw", pi, vwA, list(range(4, 16)), 4, bvw)

        def gn_fn(slot):
            for qt in range(8):
                b = next_bank()
                ps = psb[b][:, 0:48]
                tt = 8 + qt
                for kc in range(16):
                    mm(ps, hnT[:, kc, tt * 128:(tt + 1) * 128], gnw[:, kc, :], kc == 0, kc == 15,
                       ["gnw", ("hn", tt)], [("ps", b)])
                op("dve", lambda e, ps=ps: e.tensor_tensor(out=gtmp[:], in0=ps, in1=cA[:, CA_BGN:CA_BGN + 48], op=ALU.add),
                   reads=[("ps", b), "cA"], writes=["gtmp"])
                op("act", lambda e, qt=qt: e.activation(out=gsig[:, qt, :], in_=gtmp[:], func=AF.Sigmoid),
                   reads=["gtmp"], writes=[("gsig", qt)])
        jobs.append((None, 0, gn_fn))

        R3K = [("xb", 0), ("xb", 1), ("xb", 2), ("xn", 0), ("xn", 1)]
        for pi in range(8):
            kT_job("q", pi, qT, OWNCH, tshift=1024, extra_w=(R3K if pi == 0 else ()))
            if pi == 1:
                jobs.append((None, 0, lambda _: attention("pro")))

        sbank = [0]
        pbuf = [0]

        def attention(mode):

            def cmp_scores(g):
                for h in range(4):
                    for c in range(2):
                        b = sbank[0] % 2
                        sbank[0] += 1
                        ps = psb[b][0:127, 0:512]
                        mm(ps, kcbT[:, g, :], qT[:, 4 * g + h, c * 512:(c + 1) * 512], True, False,
                           [("kcbT", g), ("qT", 4 * g + h)], [("ps", b)])
                        mm(ps, cB[0:127, CB_ID:CB_ID + 127], cmpb[0:127, c * 512:(c + 1) * 512], False, True,
                           ["cB"], [("ps", b)])
                        op("act", lambda e, ps=ps, h=h, c=c: e.activation(
                            out=expT[0:127, h, c * 512:(c + 1) * 512], in_=ps, func=AF.Exp,
                            bias=cA[0:127, CA_CV:CA_CV + 1], scale=SCALE),
                           reads=[("ps", b), "cA"], writes=[("expT", h, c)] + (["bv"] if g == 0 else []))

            def cps(h):
                if h < 3:
                    return 6, psb[6][:, h * 161:(h + 1) * 161], ("ps", 6)
                return 7, psb[7][:, 0:161], ("ps", 7)

            def cmp_pv(u):
                g, qt = divmod(u, 8)
                for h in range(4):
                    b, ap, key = cps(h)
                    mm(ap, expT[0:127, h, qt * 128:(qt + 1) * 128], vcb[0:127, g, :], True, True,
                       [("expT", h, qt // 4), ("vcb", g), "vcb1"], [key])

            def smv(par, br, k):
                return sm[:, par, br, 4 * k:4 * k + 4]

            def cmp_chain(u):
                g, qt = divmod(u, 8)
                par = u % 2
                dd, rd, gr = smv(par, 0, 0), smv(par, 0, 1), smv(par, 0, 2)
                v6 = psb[6][:, 0:483].rearrange("p (h c) -> p h c", h=3)
                op("dve", lambda e: e.tensor_scalar(out=dd[:, 0:3].unsqueeze(2), in0=v6[:, :, 128:129], scalar1=1e-30,
                                                    scalar2=None, op0=ALU.max),
                   reads=[("ps", 6)], writes=[("dd", par, 0, 0)])
                op("dve", lambda e: e.tensor_scalar(out=dd[:, 3:4], in0=psb[7][:, 128:129], scalar1=1e-30,
                                                    scalar2=None, op0=ALU.max),
                   reads=[("ps", 7)], writes=[("dd", par, 0, 1)])
                op("dve", lambda e: e.reciprocal(out=rd, in_=dd), reads=[("dd", par, 0, 0), ("dd", par, 0, 1)],
                   writes=[("rd", par, 0)])
                op("dve", lambda e: e.tensor_tensor(out=gr, in0=rd, in1=gsig[:, qt, 4 * g:4 * g + 4], op=ALU.mult),
                   reads=[("rd", par, 0), ("gsig", qt)], writes=[("gr", par, 0)])
                for h in range(4):
                    b, ap, key = cps(h)
                    op("dve", lambda e, h=h, ap=ap: e.tensor_scalar(
                        out=ocomb[par][:, h * 128:(h + 1) * 128], in0=ap[:, 0:128], scalar1=gr[:, h:h + 1],
                        scalar2=None, op0=ALU.mult),
                       reads=[key, ("gr", par, 0)], writes=[("ocomb", par, h)])
                op("dve", lambda e: e.tensor_tensor(out=impw[:, 0:3, :], in0=v6[:, :, 129:161],
                                                    in1=rd[:, 0:3].unsqueeze(2).to_broadcast([128, 3, 32]), op=ALU.mult),
                   reads=[("ps", 6), ("rd", par, 0)], writes=["impw0"])
                op("dve", lambda e: e.tensor_scalar(out=impw[:, 3, :], in0=psb[7][:, 129:161], scalar1=rd[:, 3:4],
                                                    scalar2=None, op0=ALU.mult),
                   reads=[("ps", 7), ("rd", par, 0)], writes=["impw1"])
                op("dve", lambda e: e.tensor_reduce(out=imp[:, 0, :], in_=impw[:].rearrange("p h j -> p j h"),
                                                    axis=AX.X, op=ALU.add),
                   reads=["impw0", "impw1"], writes=["imp0"])
                op("dve", lambda e: e.tensor_tensor(out=imp[:, 1, :], in0=imp[:, 0, :], in1=F1e4[:, qt, :], op=ALU.max),
                   reads=["imp0", "cC"], writes=["imp1"])
                op("dve", lambda e: e.tensor_tensor(out=imp[:, 1, :], in0=imp[:, 1, :], in1=cvb[:, qt, :], op=ALU.mult),
                   reads=["imp1", "cC"], writes=["imp1"])
                op("dve", lambda e: e.tensor_tensor(out=imp[:, 1, :], in0=imp[:, 1, :], in1=cvbm1[:, qt, :], op=ALU.add),
                   reads=["imp1", "cC"], writes=["imp1"])
                op("dve", lambda e: e.max(out=top8[:], in_=imp[:, 1, :]), reads=["imp1"], writes=["top8"])
                op("dve", lambda e: e.tensor_scalar(out=imp[:, 2, :], in0=imp[:, 1, :], scalar1=top8[:, 7:8],
                                                    scalar2=None, op0=ALU.is_ge),
                   reads=["imp1", "top8"], writes=["imp2"])
                op("dve", lambda e: e.tensor_tensor(out=imp[:, 2, :], in0=imp[:, 2, :], in1=cvb[:, qt, :], op=ALU.mult),
                   reads=["imp2", "cC"], writes=["imp2"])
                op("dve", lambda e: e.tensor_scalar(out=selb[:, 0:32], in0=imp[:, 2, :], scalar1=-NEG, scalar2=NEG,
                                                    op0=ALU.mult, op1=ALU.add),
                   reads=["imp2"], writes=["selb"])

            def selb_transpose(u):
                par = u % 2
                mm(psb[7][:, 256:384], selb[:], ident, True, True, ["selb", "cB"], [("ps", 7)])
                op("act", lambda e: e.activation(out=selbT[:, par, :], in_=psb[7][:, 256:384], func=AF.Identity),
                   reads=[("ps", 7)], writes=[("selbT", par)])

            def branch_items(u, br):
                g, qt = divmod(u, 8)
                par = u % 2
                ql = 8 + qt
                kts = list(range(0, ql + 1)) if br == 1 else list(range(ql - 4, ql + 1))
                ob = (2, 3) if br == 1 else (4, 5)
                items = []
                for ki, kt in enumerate(kts):
                    def S(b, g=g, qt=qt, kt=kt, br=br, ql=ql, par=par):
                        ps = psb[b][:, 0:512].rearrange("p (h q) -> p h q", h=4)
                        rq = qT[:, 4 * g:4 * g + 4, qt * 128:(qt + 1) * 128]
                        qk = [("qT", 4 * g + h) for h in range(4)]
                        if br == 1:
                            diag = kt == ql
                            mm(ps, kTb[:, g, kt * 128:(kt + 1) * 128], rq, True, False, [("kTb", g)] + qk, [("ps", b)])
                            if not diag:
                                mm(ps, Emat[:, kt, :], selbT[:, par, :].unsqueeze(1).to_broadcast([128, 4, 128]),
                                   False, True, ["cB", ("selbT", par)], [("ps", b)])
                            if diag:
                                mm(ps, ident, tri_lo.unsqueeze(1).to_broadcast([128, 4, 128]), False, True,
                                   ["cB"], [("ps", b)])
                        else:
                            lo = kt == ql
                            hi = kt == ql - 4
                            mm(ps, kwT[:, g, (kt - 4) * 128:(kt - 3) * 128], rq, True, not (lo or hi),
                               [("kwT", g)] + qk, [("ps", b)])
                            if lo or hi:
                                mm(ps, ident, (tri_lo if lo else tri_hi).unsqueeze(1).to_broadcast([128, 4, 128]),
                                   False, True, ["cB"], [("ps", b)])

                    def E(b, p, kt=kt):
                        if kt >= 8:
                            op("act", lambda e: e.activation(out=pT[p][:], in_=psb[b][:, 0:512], func=AF.Exp,
                                                             bias=0.0, scale=SCALE),
                               reads=[("ps", b)], writes=[("pT", p)])
                            return
                        op("act", lambda e: e.activation(out=pT[p][:], in_=psb[b][:, 0:512], func=AF.Exp,
                                                         bias=cA[:, CA_KV + kt:CA_KV + kt + 1], scale=SCALE),
                           reads=[("ps", b), "cA"], writes=[("pT", p)])

                    def PV(p, g=g, kt=kt, br=br, ki=ki, n=len(kts)):
                        for h in range(4):
                            bb = ob[h // 2]
                            rhs = vsA[:, kt, g, :] if br == 1 else vwA[:, kt - 4, g, :]
                            rk = [("vs", kt, g // 2), "vs1"] if br == 1 else [("vw", kt, g // 2), "vw1"]
                            mm(psb[bb][:, (h % 2) * 129:(h % 2) * 129 + 129], pT[p][:, h * 128:(h + 1) * 128], rhs,
                               (ki == 0 and h % 2 == 0), ki == n - 1, [("pT", p)] + rk, [("ps", bb)])
                    items.append((S, E, PV))

                def fin(u=u, br=br, ob=ob, g=g, qt=qt, par=par):
                    dd, rd, gr = smv(par, br, 0), smv(par, br, 1), smv(par, br, 2)
                    for bi in range(2):
                        v = psb[ob[bi]][:, 0:258].rearrange("p (h c) -> p h c", h=2)
                        op("dve", lambda e, v=v, bi=bi: e.tensor_scalar(
                            out=dd[:, 2 * bi:2 * bi + 2].unsqueeze(2), in0=v[:, :, 128:129], scalar1=1e-30,
                            scalar2=None, op0=ALU.max),
                           reads=[("ps", ob[bi])], writes=[("dd", par, br, bi)])
                    op("dve", lambda e: e.reciprocal(out=rd, in_=dd), reads=[("dd", par, br, 0), ("dd", par, br, 1)],
                       writes=[("rd", par, br)])
                    op("dve", lambda e: e.tensor_tensor(out=gr, in0=rd, in1=gsig[:, qt, br * 16 + 4 * g:br * 16 + 4 * g + 4],
                                                        op=ALU.mult),
                       reads=[("rd", par, br), ("gsig", qt)], writes=[("gr", par, br)])
                    for h in range(4):
                        src = psb[ob[h // 2]][:, (h % 2) * 129:(h % 2) * 129 + 128]
                        if br == 1:
                            dst = ocomb[par][:, h * 128:(h + 1) * 128]
                            wk_ = [("ocomb", par, h)]
                        else:
                            dst = obf[:, h * 128:(h + 1) * 128]
                            wk_ = [("obf", h)]
                        op("dve", lambda e, h=h, src=src, dst=dst: e.scalar_tensor_tensor(
                            out=dst, in0=src, scalar=gr[:, h:h + 1], in1=ocomb[par][:, h * 128:(h + 1) * 128],
                            op0=ALU.mult, op1=ALU.add),
                           reads=[("ps", ob[h // 2]), ("gr", par, br), ("ocomb", par, h)], writes=wk_)
                return items, fin

            def o_transpose(u):
                g, qt = divmod(u, 8)
                b = sbank[0] % 2
                sbank[0] += 1
                for h in range(4):
                    mm(psb[b][:, h * 128:(h + 1) * 128], obf[:, h * 128:(h + 1) * 128], ident, True, True,
                       [("obf", h), "cB"], [("ps", b)])
                op("act", lambda e: e.activation(out=oT[:, 4 * g:4 * g + 4, qt * 128:(qt + 1) * 128],
                                                 in_=psb[b][:, 0:512].rearrange("p (h q) -> p h q", h=4), func=AF.Identity),
                   reads=[("ps", b)], writes=[("oT", 4 * g + h, qt) for h in range(4)] + [("hn", qt)])

            if mode == "pro":
                cmp_scores(0)
                cmp_pv(0)
                cmp_chain(0)
                return
            selb_transpose(0)
            G = []
            for u in range(32):
                its = []
                for br in (1, 2):
                    items, fin = branch_items(u, br)
                    for k, it in enumerate(items):
                        its.append([it[0], it[1], it[2], [fin] if k == len(items) - 1 else []])
                pre = []
                if u + 1 < 32:
                    if (u + 1) % 8 == 0:
                        pre.append(lambda u=u: cmp_scores((u + 1) // 8))
                    pre.append(lambda u=u: cmp_pv(u + 1))
                    pre.append(lambda u=u: cmp_chain(u + 1))
                its[0].append(pre)
                if u >= 1:
                    its[3][3].append(lambda u=u: o_transpose(u - 1))
                if u + 1 < 32:
                    its[min(8, len(its) - 3)][3].append(lambda u=u: selb_transpose(u + 1))
                G.extend(its)
            G[-1][3].append(lambda: o_transpose(31))
            slots = []
            n = len(G)

            def issue(j):
                slots.append((sbank[0] % 2, pbuf[0] % 3))
                sbank[0] += 1
                pbuf[0] += 1
                G[j][0](slots[j][0])
                G[j][1](slots[j][0], slots[j][1])
            issue(0)
            issue(1)
            for i in range(n):
                if len(G[i]) > 4:
                    for f in G[i][4]:
                        f()
                if i + 2 < n:
                    issue(i + 2)
                G[i][2](slots[i][1])
                for f in G[i][3]:
                    f()

        jobs.append((None, 0, lambda _: attention("main")))
        jobs.append((None, 0, lambda s: pg.barrier()))

        def za_job(pi):
            def fn(slot):
                for ct in range(2):
                    hd = pi * 2 + ct

                    def evac(ci, b, ps, t0, n, hd=hd):
                        zi = (hd * 2 + ci) % 2
                        op("act", lambda e: e.activation(out=pT[zi][:], in_=ps, func=AF.Silu, bias=bias_col("za", hd)),
                           reads=[("ps", b), "cA"], writes=[("zt", zi)])
                        op("dve", lambda e: e.tensor_tensor(out=oT[:, hd, t0 - 1024:t0 - 1024 + n],
                                                            in0=oT[:, hd, t0 - 1024:t0 - 1024 + n], in1=pT[zi][:], op=ALU.mult),
                           reads=[("zt", zi)] + [("oT", hd, q) for q in range(8)], writes=[("oaT", hd, ci)])
                    fm_tile(slot, ct * 128, OWNCH, evac)
            jobs.append((w_in[:, OFF["za"] + pi * 256:OFF["za"] + pi * 256 + 256], 256, fn))
        for pi in range(8):
            za_job(pi)

        ub = [wk[:, 0:1026], wk[:, 1026:2052]]
        yb = [wk[:, 2052:3076], wk[:, 3076:4100]]
        zs = [wk[:, 4100:4612], wk[:, 4612:5124]]

        def conv_jobs(c2):
            CH3 = [(1024, 342), (1366, 342), (1708, 342)]
            hk = [("hn", t) for t in range(8, 16)] + ["halo"]

            def fn_u(slot):
                for ct in range(2):
                    c = c2 * 2 + ct

                    def evac(ci, b, ps, t0, n, c=c, ct=ct):
                        o = t0 - 1024
                        no = min(n, 1024 - o)
                        op("act", lambda e: e.activation(out=ub[ct][:, 2 + o:2 + o + no], in_=ps[:, 0:no],
                                                         func=AF.Identity, bias=bias_col("u", c)),
                           reads=[("ps", b), "cA"], writes=[("ub", ct, ci)])
                        if no < n:
                            op("act", lambda e: e.activation(out=ub[ct][:, 0:2], in_=ps[:, no:n], func=AF.Identity,
                                                             bias=bias_col("u", c)),
                               reads=[("ps", b), "cA"], writes=[("ub", ct, 3)])
                    fm_tile(slot, ct * 128, CH3, evac, src=hnT, src_keys=hk)

            def fn_cc(slot):
                for ct in range(2):
                    c = c2 * 2 + ct

                    def evac(ci, b, ps, t0, n, c=c, ct=ct):
                        o = t0 - 1024
                        no = min(n, 1024 - o)
                        sl = ub[ct][:, 2 + o:2 + o + no]
                        op("dve", lambda e: e.scalar_tensor_tensor(out=sl, in0=ps[:, 0:no], scalar=bias_col("cc", c),
                                                                   in1=sl, op0=ALU.add, op1=ALU.mult),
                           reads=[("ps", b), "cA", ("ub", ct, ci)], writes=[("ub", ct, ci)])
                        if no < n:
                            sh = ub[ct][:, 0:2]
                            op("dve", lambda e: e.scalar_tensor_tensor(out=sh, in0=ps[:, no:n], scalar=bias_col("cc", c),
                                                                       in1=sh, op0=ALU.add, op1=ALU.mult),
                               reads=[("ps", b), "cA", ("ub", ct, 3)], writes=[("ub", ct, 3)])
                            op("dve", lambda e: e.tensor_scalar(out=sh, in0=sh, scalar1=cA[:, CA_HV:CA_HV + 1],
                                                                scalar2=None, op0=ALU.mult),
                               reads=[("ub", ct, 3), "cA"], writes=[("ub", ct, 3)])
                    fm_tile(slot, ct * 128, CH3, evac, src=hnT, src_keys=hk)
                    vk = [("ub", ct, 0), ("ub", ct, 1), ("ub", ct, 2), ("ub", ct, 3)]
                    cw = lambda k, c=c: cA[:, CA_CW + 3 * c + k:CA_CW + 3 * c + k + 1]
                    op("dve", lambda e, ct=ct, cw=cw: e.tensor_scalar(out=yb[ct], in0=ub[ct][:, 0:1024], scalar1=cw(0),
                                                                     scalar2=None, op0=ALU.mult),
                       reads=vk + ["cA"], writes=[("yb", ct)])
                    for k in (1, 2):
                        op("dve", lambda e, ct=ct, cw=cw, k=k: e.scalar_tensor_tensor(
                            out=yb[ct], in0=ub[ct][:, k:k + 1024], scalar=cw(k), in1=yb[ct], op0=ALU.mult, op1=ALU.add),
                           reads=vk + ["cA", ("yb", ct)], writes=[("yb", ct)])
                    op("dve", lambda e, ct=ct, c=c: e.tensor_scalar(out=yb[ct], in0=yb[ct],
                                                                   scalar1=cA[:, CA_CB + c:CA_CB + c + 1], scalar2=None,
                                                                   op0=ALU.add),
                       reads=[("yb", ct), "cA"], writes=[("yb", ct)])

            def fn_cb(slot):
                for ct in range(2):
                    c = c2 * 2 + ct

                    def evac(ci, b, ps, t0, n, c=c, ct=ct):
                        sl = yb[ct][:, t0 - 1024:t0 - 1024 + n]
                        op("dve", lambda e: e.scalar_tensor_tensor(out=sl, in0=ps, scalar=bias_col("cb", c), in1=sl,
                                                                   op0=ALU.add, op1=ALU.mult),
                           reads=[("ps", b), "cA", ("yb", ct)], writes=[("yb", ct)])
                    fm_tile(slot, ct * 128, OWNCH, evac)

            def fn_zc(slot):
                for ct in range(2):
                    c = c2 * 2 + ct

                    def evac(ci, b, ps, t0, n, c=c, ct=ct):
                        zi = ci
                        op("act", lambda e: e.activation(out=zs[zi], in_=ps, func=AF.Silu, bias=bias_col("zc", c)),
                           reads=[("ps", b), "cA"], writes=[("zs", zi)])
                        op("dve", lambda e: e.tensor_tensor(out=ycT[:, c, t0 - 1024:t0 - 1024 + n],
                                                            in0=yb[ct][:, t0 - 1024:t0 - 1024 + n], in1=zs[zi], op=ALU.mult),
                           reads=[("zs", zi), ("yb", ct)], writes=[("ycT", c)])
                    fm_tile(slot, ct * 128, OWNCH, evac)
            for seg, fn in (("u", fn_u), ("cc", fn_cc), ("cb", fn_cb), ("zc", fn_zc)):
                jobs.append((w_in[:, OFF[seg] + c2 * 256:OFF[seg] + c2 * 256 + 256], 256, fn))
        for c2 in range(8):
            conv_jobs(c2)
        jobs.append((None, 0, lambda s: pg.barrier()))

        g0 = wk[:, 0:2048].rearrange("p (t c n) -> p t c n", t=2, c=2)
        g1 = wk[:, 2048:4096].rearrange("p (t c n) -> p t c n", t=2, c=2)

        def merge_jobs(j2):
            def gate_fn(seg, gbuf):
                def fn(slot):
                    for ct in range(2):
                        j = j2 * 2 + ct

                        def evac(ci, b, ps, t0, n, j=j, ct=ct):
                            op("act", lambda e: e.activation(out=gbuf[:, ct, ci, :], in_=ps, func=AF.Sigmoid,
                                                             bias=bias_col(seg, j)),
                               reads=[("ps", b), "cA"], writes=[(seg, ct, ci)])
                        fm_tile(slot, ct * 128, OWNCH, evac)
                return fn

            def proj_fn(seg, gbuf, src, srck, last):
                def fn(slot):
                    for ct in range(2):
                        j = j2 * 2 + ct

                        def evac(ci, b, ps, t0, n, j=j, ct=ct):
                            op("dve", lambda e: e.tensor_tensor(out=gbuf[:, ct, ci, :], in0=ps, in1=gbuf[:, ct, ci, :],
                                                                op=ALU.mult),
                               reads=[("ps", b), (seg, ct, ci)], writes=[(seg, ct, ci)])
                            if last:
                                op("dve", lambda e: e.tensor_tensor(out=yT[:, j, t0:t0 + n], in0=g0[:, ct, ci, :],
                                                                    in1=g1[:, ct, ci, :], op=ALU.add),
                                   reads=[("gm0", ct, ci), ("gm1", ct, ci)], writes=[("yT", j)])
                        fm_tile(slot, ct * 128, [(0, 512), (512, 512)], evac, src=src, src_keys=srck)
                return fn
            oak = [("oaT", hd, ci) for hd in range(16) for ci in range(2)]
            yck = [("ycT", c) for c in range(16)]
            cs = slice(j2 * 256, j2 * 256 + 256)
            jobs.append((w_in[:, OFF["gm0"] + j2 * 256:OFF["gm0"] + j2 * 256 + 256], 256, gate_fn("gm0", g0)))
            jobs.append((w_in[:, OFF["gm1"] + j2 * 256:OFF["gm1"] + j2 * 256 + 256], 256, gate_fn("gm1", g1)))
            jobs.append((p_attn[:, cs], 256, proj_fn("gm0", g0, oT, oak, False)))
            jobs.append((p_conv[:, cs], 256, proj_fn("gm1", g1, ycT, yck, True)))
        for j2 in range(8):
            merge_jobs(j2)
        jobs.append((None, 0, lambda s: pg.barrier()))

        def load_resid(_):
            xo = xctx[1024:2048, :].rearrange("(q p) n -> p q n", p=128)
            load_xblock(0)
        jobs.append((None, 0, load_resid))

        def load_xblock(pj, gate=()):
            xo = xctx[1024:2048, :].rearrange("(q p) n -> p q n", p=128)
            op("sp", lambda e, pj=pj: e.dma_start(out=resid[:, :, pj * 256:(pj + 1) * 256],
                                                  in_=xo[:, :, pj * 256:(pj + 1) * 256]),
               reads=list(gate), writes=[("res", qt, pj) for qt in range(8)], dma_slot=("res", pj))

        def final_a(qt):
            rk = [("res", qt, pj) for pj in range(8)]
            op("act", lambda e, qt=qt: e.activation(out=wk[:, 0:2048], in_=resid[:, qt, :], func=AF.Square,
                                                    accum_out=stat[:, 48 + qt:49 + qt]),
               reads=rk, writes=["sqf", ("fss", qt)])
            op("act", lambda e, qt=qt: e.activation(out=stat[:, 56 + qt:57 + qt], in_=stat[:, 48 + qt:49 + qt],
                                                    func=AF.Sqrt, scale=1.0 / D, bias=EPS),
               reads=[("fss", qt)], writes=[("frms", qt)])

        def final_b(qt):
            rk = [("res", qt, pj) for pj in range(8)]
            op("dve", lambda e, qt=qt: e.reciprocal(out=stat[:, 40 + qt:41 + qt], in_=stat[:, 56 + qt:57 + qt]),
               reads=[("frms", qt)], writes=[("frstd", qt)])
            op("dve", lambda e, qt=qt: e.scalar_tensor_tensor(out=resid[:, qt, :], in0=resid[:, qt, :],
                                                              scalar=stat[:, 40 + qt:41 + qt], in1=fgb,
                                                              op0=ALU.mult, op1=ALU.mult),
               reads=rk + [("frstd", qt), "fg"], writes=rk)
            op("sp", lambda e, qt=qt: e.dma_start(out=outd[qt * 128:(qt + 1) * 128, :], in_=resid[:, qt, :]),
               reads=rk, dma_slot=("out", qt))

        def wo_job(pj):
            def fn(slot):
                g_ = [("pan", 1 - slot)]
                if pj + 1 < 8:
                    load_xblock(pj + 1, gate=g_)
                if pj == 3:
                    op("sp", lambda e: e.dma_start(out=fgb, in_=fgd), reads=g_, writes=["fg"], dma_slot="fg")
                for qt in range(8):
                    b = next_bank()
                    ps = psb[b][:, 0:256]
                    for kc in range(16):
                        mm(ps, yT[:, kc, qt * 128:(qt + 1) * 128], pan[slot][:, kc, 0:256], kc == 0, kc == 15,
                           [("pan", slot), ("yT", kc)], [("ps", b)])
                    sl = resid[:, qt, pj * 256:(pj + 1) * 256]
                    op("dve", lambda e, sl=sl, ps=ps: e.tensor_tensor(out=sl, in0=ps, in1=sl, op=ALU.add),
                       reads=[("ps", b), ("res", qt, pj)], writes=[("res", qt, pj)])
                    if pj == 7:
                        final_a(qt)
                        if qt >= 1:
                            final_b(qt - 1)
                        if qt == 7:
                            final_b(7)
            jobs.append((w_o[:, pj * 256:(pj + 1) * 256], 256, fn))
        for pj in range(8):
            wo_job(pj)

        fws = [("out", qt) for qt in range(8)]
        if dbg is not None:
            del jobs[dbg["stop"]:]
            views = dict(hnT=hnT[:], qT=qT, kTb=kTb, kwT=kwT, vsA=vsA, vwA=vwA, kcbT=kcbT[:], vcb=vcb[:], gsig=gsig[:],
                         oT=oT, ycT=ycT, yT=yT, resid=resid, cB=cB[:], hid=hid[:], expT=expT[:], selbT=selbT[:],
                         imp=imp[:], stat=stat[:])
            fws = []

            def dump(_):
                pg.barrier()
                for name in dbg["dump"]:
                    v = views[name]
                    dd = nc.dram_tensor("dbg_" + name, list(v.shape), v.dtype, kind="ExternalOutput").ap()
                    op("sp", lambda e, v=v, dd=dd: e.dma_start(out=dd, in_=v), dma_slot=("dbg", name))
                    fws.append(("dbg", name))
            jobs.append((None, 0, dump))
        run_jobs()
        pg.emit(final_wait_slots=fws)
    return nc


def _host_consts(h):
    i = np.arange(OWN)
    t_glob = i + (OWN if h == 1 else 0)
    j = np.arange(32)
    jg = j - (0 if h == 1 else 16)
    valid = jg >= 0
    causal = (jg[None, :] * 64 <= t_glob[:, None]) & valid[None, :]
    cur = t_glob // 64
    forced = ((jg[None, :] == 0) | (jg[None, :] == cur[:, None]) | (jg[None, :] == cur[:, None] - 1)) & causal
    to_t = lambda a: np.ascontiguousarray(a.reshape(8, 128, 32).transpose(1, 0, 2).reshape(128, 256))
    cC = np.concatenate([to_t(forced.astype(np.float32) * 8192.0), to_t(causal.astype(np.float32)),
                         to_t(causal.astype(np.float32) - 1.0)], axis=1).astype(np.float32)
    kvalid = np.zeros((128, 16), np.float32)
    cvalid = np.zeros((128, 1), np.float32)
    if h == 0:
        kvalid[:, 0:8] = NEG
        cvalid[0:64] = NEG
    hv = np.full((128, 1), float(h), np.float32)
    p = np.arange(128)
    ident = np.eye(128, dtype=np.float32)
    tri_lo = np.where(p[:, None] <= p[None, :], 0.0, NEG).astype(np.float32)
    tri_hi = np.where(p[:, None] > p[None, :], 0.0, NEG).astype(np.float32)
    E = np.zeros((128, 16, 128), np.float32)
    for kt in range(16):
        for k in range(128):
            E[2 * kt + k // 64, kt, k] = 1.0
    n = np.arange(128)
    cmpb = np.where(16 * n[:, None] + 31 <= 1024 + i[None, :], 0.0, NEG).astype(np.float32)
    c0 = (np.arange(128) * 16)[:, None]
    s0 = (np.arange(32) * 64)[None, :]
    agg = (np.maximum(0, np.minimum(c0 + 32, s0 + 64) - np.maximum(c0, s0)) / 32.0).astype(np.float32)
    cB = np.concatenate([ident, tri_lo, tri_hi, E.reshape(128, 2048), cmpb, agg], axis=1).astype(np.float32)
    return cC, kvalid, cvalid, hv, cB


_NC_CACHE = {}


def kernel(x, norm_g, w_in, b_in, cmp_pe_k, cmp_w1_k, cmp_w2_k, cmp_pe_v, cmp_w1_v, cmp_w2_v,
           conv_w, conv_b, p_attn, p_conv, w_o, final_g):
    f = lambda a: np.ascontiguousarray(np.asarray(a, dtype=np.float32))
    x = f(x)
    w_in0 = f(w_in)[0]
    b = f(b_in)[0]
    ng = f(norm_g)[0]
    cwv = f(conv_w)[0]
    cbv = f(conv_b)[0]
    bfm = np.zeros((128, NFM), np.float32)
    for seg, cnt in FMSEG:
        for t in range(cnt):
            bfm[:, FMB[seg] + t] = b[OFF[seg] + t * 128:OFF[seg] + (t + 1) * 128]
    ngc = ng.reshape(16, 128).T
    cwc = cwv.reshape(3, 16, 128).transpose(2, 1, 0).reshape(128, 48)
    cbc = cbv.reshape(16, 128).T
    pek = f(cmp_pe_k)[0].T
    pev = f(cmp_pe_v)[0].T
    bgn = np.broadcast_to(b[OFF["gn"]:OFF["gn"] + 48], (128, 48))
    w2 = np.concatenate([f(cmp_w2_k)[0].reshape(2, 128, 128).transpose(1, 0, 2).reshape(128, 256),
                         f(cmp_w2_v)[0].reshape(2, 128, 128).transpose(1, 0, 2).reshape(128, 256)], axis=1)
    bv = np.concatenate([np.broadcast_to(b[OFF["vs"]:OFF["vs"] + 512], (128, 512)),
                         np.broadcast_to(b[OFF["vw"]:OFF["vw"] + 512], (128, 512))], axis=1)
    fg = np.broadcast_to(f(final_g), (128, D))
    shared = dict(w_in=w_in0, p_attn=f(p_attn)[0], p_conv=f(p_conv)[0], w_o=f(w_o)[0],
                  w1k=f(cmp_w1_k)[0], w1v=f(cmp_w1_v)[0], w2=np.ascontiguousarray(w2),
                  bv=np.ascontiguousarray(bv), fg=np.ascontiguousarray(fg))
    hc = [_host_consts(0), _host_consts(1)]
    in_maps = []
    for c in range(8):
        bi, h = divmod(c, 2)
        cC, kvalid, cvalid, hv, cB = hc[h]
        cA = np.concatenate([bfm, ngc, cwc, cbc, kvalid, cvalid, hv, pek, pev, bgn], axis=1).astype(np.float32)
        assert cA.shape == (128, CA_N)
        if h == 1:
            xc = x[bi]
        else:
            xc = np.concatenate([np.zeros((OWN, D), np.float32), x[bi, 0:OWN]], axis=0)
        m = dict(shared)
        m.update(xctx=np.ascontiguousarray(xc), cA=np.ascontiguousarray(cA), cB=cB, cC=cC)
        in_maps.append(m)
    if "nc" not in _NC_CACHE:
        _NC_CACHE["nc"] = build_program()
    res = run_bass_kernel_spmd(_NC_CACHE["nc"], in_maps, core_ids=list(range(8)))
    out = np.zeros((4, S, D), np.float32)
    for c in range(8):
        bi, h = divmod(c, 2)
        out[bi, h * OWN:(h + 1) * OWN] = res.results[c]["out"]
    return out
```

```python
import contextlib
import numpy as np
import concourse.bass as bass
import concourse.mybir as mybir
from concourse.bass_utils import run_bass_kernel_spmd

F32 = mybir.dt.float32
BF16 = mybir.dt.bfloat16
AF = mybir.ActivationFunctionType
ALU = mybir.AluOpType
AX = mybir.AxisListType

D = 2048
S = 2048
OWN = 1024
N_IN = 19504
SCALE = 128 ** -0.5
NEG = -30000.0
EPS = 1e-6
OFF = dict(q=0, kc=2048, vc=2560, ks=3072, vs=3584, kw=4096, vw=4608, gn=5120, za=5168,
           u=7216, cc=9264, cb=11312, zc=13360, gm0=15408, gm1=17456)
FMSEG = [("q", 16), ("kc", 4), ("vc", 4), ("ks", 4), ("kw", 4), ("za", 16), ("u", 16), ("cc", 16),
         ("cb", 16), ("zc", 16), ("gm0", 16), ("gm1", 16)]
FMB = {}
_t = 0
for _n, _c in FMSEG:
    FMB[_n] = _t
    _t += _c
NFM = _t
CA_NG = NFM
CA_CW = CA_NG + 16
CA_CB = CA_CW + 48
CA_KV = CA_CB + 16
CA_CV = CA_KV + 16
CA_HV = CA_CV + 1
CA_PEK = CA_HV + 1
CA_PEV = CA_PEK + 32
CA_BGN = CA_PEV + 32
CA_N = CA_BGN + 48
CB_ID = 0
CB_TLO = 128
CB_THI = 256
CB_E = 384
CB_CMP = CB_E + 2048
CB_AGG = CB_CMP + 1024
CB_N = CB_AGG + 32

ENGS = ("pe", "act", "dve", "pool", "sp")


class Prog:
    def __init__(self, nc):
        self.nc = nc
        self.ops = {e: [] for e in ENGS}
        self.last_w = {}
        self.readers = {}
        self.clock = {e: {} for e in ENGS}
        self.dma_slots = {}

    def _need(self, eng, tok, waits):
        src, idx = tok
        if src == eng and eng == "pe":
            return
        if self.clock[eng].get(src, 0) >= idx:
            return
        waits[src] = max(waits.get(src, 0), idx)

    def op(self, eng, fn, reads=(), writes=(), dma_slot=None, ndma=1):
        waits = {}
        for k in reads:
            t = self.last_w.get(k)
            if t is not None:
                self._need(eng, t, waits)
        for k in writes:
            t = self.last_w.get(k)
            if t is not None:
                self._need(eng, t, waits)
            for t in self.readers.get(k, ()):
                self._need(eng, t, waits)
        lst = self.ops[eng]
        idx = len(lst) + 1
        if dma_slot is not None:
            self.dma_slots[dma_slot] = self.dma_slots.get(dma_slot, 0) + ndma
            tok = (("dma", dma_slot), self.dma_slots[dma_slot])
        else:
            tok = (eng, idx)
        ck = self.clock[eng]
        for src, v in waits.items():
            ck[src] = max(ck.get(src, 0), v)
            if isinstance(src, str):
                pc = self.ops[src][v - 1]["clock"]
                for s2, v2 in pc.items():
                    if s2 != eng:
                        ck[s2] = max(ck.get(s2, 0), v2)
        lst.append(dict(fn=fn, waits=waits, tok=tok, clock=dict(ck), dma_slot=dma_slot, ndma=ndma))
        for k in reads:
            self.readers.setdefault(k, []).append(tok)
        for k in writes:
            self.last_w[k] = tok
            self.readers[k] = []
        return tok

    def barrier(self):
        toks = []
        for e in ENGS:
            for i in range(len(self.ops[e]) - 1, -1, -1):
                r = self.ops[e][i]
                if r["fn"] is not None and r["dma_slot"] is None:
                    toks.append((e, i + 1))
                    break
        for s, c in self.dma_slots.items():
            toks.append((("dma", s), c))
        for e in ENGS:
            waits = {}
            for t in toks:
                if t[0] == e and e == "pe":
                    continue
                self._need(e, t, waits)
            ck = self.clock[e]
            for src, v in waits.items():
                ck[src] = max(ck.get(src, 0), v)
            self.ops[e].append(dict(fn=None, waits=waits, tok=(e, len(self.ops[e]) + 1),
                                    clock=dict(ck), dma_slot=None, ndma=0))

    def emit(self, final_wait_slots=()):
        nc = self.nc
        need = {e: set() for e in ENGS}
        for e in ENGS:
            for r in self.ops[e]:
                for src, v in r["waits"].items():
                    if isinstance(src, str):
                        need[src].add(v)
        rank = {}
        for e in ENGS:
            for i, v in enumerate(sorted(need[e])):
                rank[(e, v)] = i + 1
        with contextlib.ExitStack() as st:
            esem = {e: st.enter_context(nc.semaphore("s_" + e)) for e in ENGS}
            dsem = {}
            for i, s in enumerate(self.dma_slots):
                dsem[s] = st.enter_context(nc.semaphore("d_%d" % i))
            block = st.enter_context(nc.Block())

            def run(e, engobj):
                for i, r in enumerate(self.ops[e]):
                    for src, v in r["waits"].items():
                        if isinstance(src, str):
                            engobj.wait_ge(esem[src], rank[(src, v)])
                        else:
                            engobj.wait_ge(dsem[src[1]], 16 * v)
                    if r["fn"] is None:
                        assert (e, i + 1) not in rank
                        continue
                    ins = r["fn"](engobj)
                    if r["dma_slot"] is not None:
                        insl = ins if isinstance(ins, (list, tuple)) else [ins]
                        assert len(insl) == r["ndma"]
                        for x in insl:
                            x.then_inc(dsem[r["dma_slot"]], 16)
                        assert (e, i + 1) not in rank
                    elif (e, i + 1) in rank:
                        ins.then_inc(esem[e], 1)
                if e == "sp":
                    for s in final_wait_slots:
                        engobj.wait_ge(dsem[s], 16 * self.dma_slots[s])

            @block.tensor
            def _(eng):
                run("pe", eng)

            @block.scalar
            def _(eng):
                run("act", eng)

            @block.vector
            def _(eng):
                run("dve", eng)

            @block.gpsimd
            def _(eng):
                run("pool", eng)

            @block.sync
            def _(eng):
                run("sp", eng)


def build_program(dbg=None):
    nc = bass.Bass("TRN2", target_bir_lowering=False)
    dt_in = lambda n, s: nc.dram_tensor(n, s, F32, kind="ExternalInput").ap()
    xctx = dt_in("xctx", [S, D])
    w_in = dt_in("w_in", [D, N_IN])
    p_attn = dt_in("p_attn", [D, D])
    p_conv = dt_in("p_conv", [D, D])
    w_o = dt_in("w_o", [D, D])
    w1k = dt_in("w1k", [4096, 256])
    w1v = dt_in("w1v", [4096, 256])
    w2d = dt_in("w2", [128, 4 * 128])
    cAd = dt_in("cA", [128, CA_N])
    cBd = dt_in("cB", [128, CB_N])
    cCd = dt_in("cC", [128, 768])
    bvd = dt_in("bv", [128, 1024])
    fgd = dt_in("fg", [128, D])
    outd = nc.dram_tensor("out", [OWN, D], F32, kind="ExternalOutput").ap()

    with contextlib.ExitStack() as st:
        T = lambda name, shape, dt: st.enter_context(nc.sbuf_tensor("sb_" + name, shape, dt))
        hnT = T("hnT", [128, 16, S + 2], BF16)
        R3 = T("R3", [128, 16384], BF16)
        KV = T("KV", [128, 28784], BF16)
        pan = [T("pan%d" % i, [128, 16, 256], BF16) for i in range(2)]
        expT = T("expT", [128, 4, OWN], BF16)
        cA = T("cA", [128, CA_N], F32)
        cB = T("cB", [128, CB_N], BF16)
        cC = T("cC", [128, 768], F32)
        w2 = T("w2", [128, 4, 128], BF16)
        peT = T("peT", [128, 64], BF16)
        kcbT = T("kcbT", [128, 4, 127], BF16)
        vcb = T("vcb", [128, 4, 161], BF16)
        hid = T("hid", [128, 4, 2, 127], BF16)
        hb = T("hb", [128, 2], F32)
        gsig = T("gsig", [128, 8, 48], F32)
        gtmp = T("gtmp", [128, 48], F32)
        gnw = T("gnw", [128, 16, 48], BF16)
        stat = T("stat", [128, 64], F32)
        pT = [T("pT%d" % i, [128, 512], BF16) for i in range(3)]
        ocomb = [T("ocomb%d" % i, [128, 512], F32) for i in range(2)]
        obf = T("obf", [128, 512], BF16)
        sm = T("sm", [128, 2, 3, 12], F32)
        impw = T("impw", [128, 4, 32], F32)
        imp = T("imp", [128, 4, 32], F32)
        top8 = T("top8", [128, 8], F32)
        selb = T("selb", [128, 128], BF16)
        selbT = T("selbT", [128, 2, 128], BF16)
        psb = [st.enter_context(nc.psum_tensor("psum%d" % i, [128, 512], F32)) for i in range(8)]

        pg = Prog(nc)
        op = pg.op

        oT = hnT[:, :, 0:OWN]
        resid = hnT[:].bitcast(F32).rearrange("p (a two) c -> p a (two c)", two=2)[:, :, 0:2048]
        R3f = R3[:].bitcast(F32)
        xb = [R3f[:, 0:2048], R3f[:, 2048:4096], R3f[:, 4096:6144]]
        xn = [R3[:, 12288:14336], R3[:, 14336:16384]]
        qT = R3[:].rearrange("p (h t) -> p h t", h=16)
        ycT = qT
        fgb = R3f[:, 0:2048]
        kTb = KV[:, 0:8192].rearrange("p (g t) -> p g t", g=4)
        kwT = KV[:, 8192:14336].rearrange("p (g t) -> p g t", g=4)
        vsA = KV[:, 14336:22592].rearrange("p (t g c) -> p t g c", t=16, g=4)
        vwA = KV[:, 22592:28784].rearrange("p (t g c) -> p t g c", t=12, g=4)
        yT = KV[:, 0:16384].rearrange("p (j t) -> p j t", j=16)
        wk = KV[:, 16384:28784].bitcast(F32)
        ef = expT[:].rearrange("p a b -> p (a b)").bitcast(F32)
        bvs = ef[:, 0:512]
        bvw = ef[:, 512:1024]
        ident = cB[:, CB_ID:CB_ID + 128]
        tri_lo = cB[:, CB_TLO:CB_TLO + 128]
        tri_hi = cB[:, CB_THI:CB_THI + 128]
        Emat = cB[:, CB_E:CB_E + 2048].rearrange("p (k c) -> p k c", k=16)
        cmpb = cB[:, CB_CMP:CB_CMP + 1024]
        F1e4 = cC[:, 0:256].rearrange("p (q j) -> p q j", q=8)
        cvb = cC[:, 256:512].rearrange("p (q j) -> p q j", q=8)
        cvbm1 = cC[:, 512:768].rearrange("p (q j) -> p q j", q=8)

        def psbf(b):
            return psb[b][:].bitcast(BF16)

        def load_consts():
            op("sp", lambda e: e.dma_start(out=cA[:], in_=cAd), writes=["cA"], dma_slot="cA")
            op("pool", lambda e: e.dma_start(out=cB[:], in_=cBd), writes=["cB"], dma_slot="cB")
            op("sp", lambda e: e.dma_start(out=cC[:], in_=cCd), writes=["cC"], dma_slot="cC")
            op("sp", lambda e: e.dma_start(out=ef[:, 0:1024], in_=bvd), writes=["bv"], dma_slot="bv")
            op("pool", lambda e: e.dma_start(out=w2[:].rearrange("p a b -> p (a b)"), in_=w2d), writes=["w2"],
               dma_slot="w2")
            op("pool", lambda e: e.dma_start(out=peT[:], in_=cAd[:, CA_PEK:CA_PEK + 64]), writes=["peT"],
               dma_slot="peT")
            gv = w_in[:, OFF["gn"]:OFF["gn"] + 48].rearrange("(kc p) n -> p kc n", p=128)
            op("pool", lambda e: [e.dma_start(out=gnw[:, 0:8, :], in_=gv[:, 0:8, :]),
                                  e.dma_start(out=gnw[:, 8:16, :], in_=gv[:, 8:16, :])],
               writes=["gnw"], dma_slot="gnw", ndma=2)
        def init_small():
            op("dve", lambda e: e.memset(vsA[:, :, :, 128:129], 1.0), writes=["vs1"])
            op("dve", lambda e: e.memset(vwA[:, :, :, 128:129], 1.0), writes=["vw1"])
            op("dve", lambda e: e.memset(vcb[:, :, 128:129], 1.0), writes=["vcb1"])
            op("dve", lambda e: e.memset(selb[:], 0.0), writes=["selb"])
            for g in range(4):
                op("dve", lambda e, g=g: e.tensor_copy(out=vcb[:, g, 129:161], in_=cB[:, CB_AGG:CB_AGG + 32]),
                   reads=["cB"], writes=["vcb1"])

        jobs = []

        def run_jobs():
            pj = [i for i, j in enumerate(jobs) if j[0] is not None]
            loaded = 0
            slot_of = {}

            def load(k):
                i = pj[k]
                src, ncols, _ = jobs[i]
                s = k % 2
                slot_of[i] = s
                v = src.rearrange("(kc p) n -> p kc n", p=128)
                op("pool", lambda e: [e.dma_start(out=pan[s][:, 0:8, 0:ncols], in_=v[:, 0:8, :]),
                                      e.dma_start(out=pan[s][:, 8:16, 0:ncols], in_=v[:, 8:16, :])],
                   reads=([("xt", 2)] if k < 2 else []),
                   writes=[("pan", s)], dma_slot=("pan", s), ndma=2)

            for i, (src, ncols, fn) in enumerate(jobs):
                if src is not None:
                    k = pj.index(i)
                    while loaded < min(len(pj), k + 2):
                        load(loaded)
                        loaded += 1
                    fn(slot_of[i])
                else:
                    fn(None)

        bank_ctr = [0]

        def next_bank(nb=8):
            b = bank_ctr[0] % nb
            bank_ctr[0] += 1
            return b

        def hn_keys(t0, n):
            return [("hn", t) for t in range(t0 // 128, (t0 + n + 127) // 128)]

        def mm(out, lhsT, rhs, start, stop, reads, writes):
            op("pe", lambda e: e.matmul(out, lhsT=lhsT, rhs=rhs, start=start, stop=stop, skip_group_check=True),
               reads=reads, writes=writes)

        def fm_tile(slot, coff, chunks, evac, src=None, src_keys=None, nb=8):
            for ci, (t0, n) in enumerate(chunks):
                b = next_bank(nb)
                ps = psb[b][:, 0:n]
                for kc in range(16):
                    if src is None:
                        rhs = hnT[:, kc, t0:t0 + n]
                        rk = hn_keys(t0, n)
                    else:
                        rhs = src[:, kc, t0:t0 + n]
                        rk = src_keys
                    mm(ps, pan[slot][:, kc, coff:coff + 128], rhs, kc == 0, kc == 15,
                       [("pan", slot)] + rk, [("ps", b)])
                evac(ci, b, ps, t0, n)

        def bias_col(seg, t):
            return cA[:, FMB[seg] + t:FMB[seg] + t + 1]

        def stage1(tt):
            x_ = xb[tt % 3]
            xn_ = xn[tt % 2]
            xk = ("xb", tt % 3)
            op("sp", lambda e, tt=tt, x_=x_: e.dma_start(out=x_, in_=xctx[tt * 128:(tt + 1) * 128, :]),
               writes=[xk, ("xt", tt)], dma_slot=xk)
            op("act", lambda e, tt=tt, x_=x_, xn_=xn_: e.activation(out=xn_, in_=x_, func=AF.Square,
                                                                   accum_out=stat[:, tt:tt + 1]),
               reads=[xk], writes=[("xn", tt % 2), ("ss", tt)])
            op("act", lambda e, tt=tt: e.activation(out=stat[:, 16 + tt:17 + tt], in_=stat[:, tt:tt + 1],
                                                    func=AF.Sqrt, scale=1.0 / D, bias=EPS),
               reads=[("ss", tt)], writes=[("rms", tt)])
            op("dve", lambda e, tt=tt: e.reciprocal(out=stat[:, 32 + tt:33 + tt], in_=stat[:, 16 + tt:17 + tt]),
               reads=[("rms", tt)], writes=[("rstd", tt)])
            op("dve", lambda e, tt=tt, x_=x_, xn_=xn_: e.tensor_scalar(
                out=xn_, in0=x_, scalar1=stat[:, 32 + tt:33 + tt], scalar2=None, op0=ALU.mult),
               reads=[xk, ("rstd", tt)], writes=[("xn", tt % 2)])

        def stage2(tt):
            xn_ = xn[tt % 2]
            for half in range(2):
                b = next_bank()
                pv = psbf(b)
                for k8 in range(8):
                    kc = half * 8 + k8
                    op("pe", lambda e, pv=pv, k8=k8, kc=kc, xn_=xn_: e.transpose(
                        out=pv[:, k8 * 128:(k8 + 1) * 128], in_=xn_[:, kc * 128:(kc + 1) * 128], identity=ident),
                       reads=[("xn", tt % 2), "cB"], writes=[("ps", b)])
                op("dve", lambda e, pv=pv, half=half, tt=tt: e.tensor_tensor(
                    out=hnT[:, half * 8:half * 8 + 8, tt * 128:(tt + 1) * 128],
                    in0=pv.rearrange("p (k t) -> p k t", k=8),
                    in1=cA[:, CA_NG + half * 8:CA_NG + half * 8 + 8].unsqueeze(2).to_broadcast([128, 8, 128]),
                    op=ALU.mult),
                   reads=[("ps", b), "cA"], writes=[("hn", tt)])

        stage1(0)
        load_consts()
        init_small()
        for tt in range(16):
            if tt + 1 < 16:
                stage1(tt + 1)
            stage2(tt)
        op("dve", lambda e: e.tensor_copy(out=hnT[:, :, 2048:2050], in_=hnT[:, :, 1022:1024]), reads=[("hn", 7)], writes=["halo"])

        ALLCH = [(0, 512), (512, 512), (1024, 512), (1536, 512)]
        OWNCH = [(1024, 512), (1536, 512)]

        def kT_job(seg, pi, dst, chunks, tshift=0, key=None, extra_w=()):
            key = key or (seg + "T")

            def fn(slot):
                for ct in range(2):
                    t = pi * 2 + ct

                    def evac(ci, b, ps, t0, n, t=t):
                        op("act", lambda e: e.activation(out=dst[:, t, t0 - tshift:t0 - tshift + n], in_=ps,
                                                         func=AF.Identity, bias=bias_col(seg, t)),
                           reads=[("ps", b), "cA"], writes=[(key, t)] + list(extra_w))
                    fm_tile(slot, ct * 128, chunks, evac)
            jobs.append((w_in[:, OFF[seg] + pi * 256:OFF[seg] + pi * 256 + 256], 256, fn))

        def compress_jobs(kind, w1d):
            src_key = [("kTb", g) for g in range(4)]
            pe_off = 0 if kind == "k" else 32

            def region(g, hc):
                r = g * 2 + hc
                return 4 + r // 4, (r % 4) * 127

            def fn_factory(half):
                def fn(slot):
                    for il in range(16):
                        i = half * 16 + il
                        for hc in range(2):
                            mm(psb[6][:, hc:hc + 1], pan[slot][:, il, hc * 128:(hc + 1) * 128],
                               peT[:, pe_off + i:pe_off + i + 1], (i == 0 and hc == 0), i == 31,
                               [("pan", slot), "peT"], [("ps", 6)])
                        for g in range(4):
                            for hc in range(2):
                                b, c = region(g, hc)
                                mm(psb[b][:, c:c + 127], pan[slot][:, il, hc * 128:(hc + 1) * 128],
                                   kTb[:, g, i:i + 16 * 126 + 1:16], (i == 0 and c == 0), i == 31,
                                   [("pan", slot)] + src_key, [("ps", b)])
                    if half == 1:
                        op("dve", lambda e: e.tensor_copy(out=hb[:], in_=psb[6][:, 0:2]), reads=[("ps", 6)], writes=["hb"])
                        for g in range(4):
                            for hc in range(2):
                                b, c = region(g, hc)
                                op("act", lambda e, g=g, hc=hc, b=b, c=c: e.activation(
                                    out=hid[:, g, hc, :], in_=psb[b][:, c:c + 127], func=AF.Silu, bias=hb[:, hc:hc + 1]),
                                   reads=[("ps", b), "hb"], writes=[("hid", g, hc)])
                        wb = 0 if kind == "k" else 2
                        for g in range(4):
                            for hc in range(2):
                                if kind == "k":
                                    mm(psb[g][:, 0:127], w2[:, wb + hc, :], hid[:, g, hc, :], hc == 0, hc == 1,
                                       ["w2", ("hid", g, hc)], [("ps", g)])
                                else:
                                    mm(psb[g][0:127, 0:128], hid[:, g, hc, :], w2[:, wb + hc, :], hc == 0, hc == 1,
                                       ["w2", ("hid", g, hc)], [("ps", g)])
                            if kind == "k":
                                op("dve", lambda e, g=g: e.tensor_copy(out=kcbT[:, g, :], in_=psb[g][:, 0:127]),
                                   reads=[("ps", g)], writes=[("kcbT", g)])
                            else:
                                op("dve", lambda e, g=g: e.tensor_copy(out=vcb[0:127, g, 0:128], in_=psb[g][0:127, 0:128]),
                                   reads=[("ps", g)], writes=[("vcb", g)])
                return fn
            jobs.append((w1d[0:2048, :], 256, fn_factory(0)))
            jobs.append((w1d[2048:4096, :], 256, fn_factory(1)))

        for pi in range(2):
            kT_job("kc", pi, kTb, ALLCH, key="kTb")
        compress_jobs("k", w1k)
        for pi in range(2):
            kT_job("vc", pi, kTb, ALLCH, key="kTb")
        compress_jobs("v", w1v)
        for pi in range(2):
            kT_job("ks", pi, kTb, ALLCH, key="kTb")
        for pi in range(2):
            kT_job("kw", pi, kwT, ALLCH[1:], tshift=512)

        def v_job(seg, pi, dstA, tiles, tile_shift, bv):
            def fn(slot):
                for tt in tiles:
                    b = next_bank()
                    ps = psb[b][:, 0:256]
                    for kc in range(16):
                        mm(ps, hnT[:, kc, tt * 128:(tt + 1) * 128], pan[slot][:, kc, 0:256], kc == 0, kc == 15,
                           [("pan", slot), ("hn", tt)], [("ps", b)])
                    op("dve", lambda e, tt=tt, ps=ps: e.tensor_tensor(
                        out=dstA[:, tt - tile_shift, pi * 2:pi * 2 + 2, 0:128],
                        in0=ps.rearrange("p (g d) -> p g d", g=2),
                        in1=bv[:, pi * 256:(pi + 1) * 256].rearrange("p (g d) -> p g d", g=2), op=ALU.add),
                       reads=[("ps", b), "bv"], writes=[(seg, tt, pi)])
            jobs.append((w_in[:, OFF[seg] + pi * 256:OFF[seg] + pi * 256 + 256], 256, fn))

        for pi in range(2):
            v_job("vs", pi, vsA, list(range(16)), 0, bvs)
        for pi in range(2):
            v_job("vw", pi, vwA, list(range(4, 16)), 4, bvw)

        def gn_fn(slot):
            for qt in range(8):
                b = next_bank()
                ps = psb[b][:, 0:48]
                tt = 8 + qt
                for kc in range(16):
                    mm(ps, hnT[:, kc, tt * 128:(tt + 1) * 128], gnw[:, kc, :], kc == 0, kc == 15,
                       ["gnw", ("hn", tt)], [("ps", b)])
                op("dve", lambda e, ps=ps: e.tensor_tensor(out=gtmp[:], in0=ps, in1=cA[:, CA_BGN:CA_BGN + 48], op=ALU.add),
                   reads=[("ps", b), "cA"], writes=["gtmp"])
                op("act", lambda e, qt=qt: e.activation(out=gsig[:, qt, :], in_=gtmp[:], func=AF.Sigmoid),
                   reads=["gtmp"], writes=[("gsig", qt)])
        jobs.append((None, 0, gn_fn))

        R3K = [("xb", 0), ("xb", 1), ("xb", 2), ("xn", 0), ("xn", 1)]
        for pi in range(8):
            kT_job("q", pi, qT, OWNCH, tshift=1024, extra_w=(R3K if pi == 0 else ()))
            if pi == 1:
                jobs.append((None, 0, lambda _: attention("pro")))

        sbank = [0]
        pbuf = [0]

        def attention(mode):

            def cmp_scores(g):
                for h in range(4):
                    for c in range(2):
                        b = sbank[0] % 2
                        sbank[0] += 1
                        ps = psb[b][0:127, 0:512]
                        mm(ps, kcbT[:, g, :], qT[:, 4 * g + h, c * 512:(c + 1) * 512], True, False,
                           [("kcbT", g), ("qT", 4 * g + h)], [("ps", b)])
                        mm(ps, cB[0:127, CB_ID:CB_ID + 127], cmpb[0:127, c * 512:(c + 1) * 512], False, True,
                           ["cB"], [("ps", b)])
                        op("act", lambda e, ps=ps, h=h, c=c: e.activation(
                            out=expT[0:127, h, c * 512:(c + 1) * 512], in_=ps, func=AF.Exp,
                            bias=cA[0:127, CA_CV:CA_CV + 1], scale=SCALE),
                           reads=[("ps", b), "cA"], writes=[("expT", h, c)] + (["bv"] if g == 0 else []))

            def cps(h):
                if h < 3:
                    return 6, psb[6][:, h * 161:(h + 1) * 161], ("ps", 6)
                return 7, psb[7][:, 0:161], ("ps", 7)

            def cmp_pv(u):
                g, qt = divmod(u, 8)
                for h in range(4):
                    b, ap, key = cps(h)
                    mm(ap, expT[0:127, h, qt * 128:(qt + 1) * 128], vcb[0:127, g, :], True, True,
                       [("expT", h, qt // 4), ("vcb", g), "vcb1"], [key])

            def smv(par, br, k):
                return sm[:, par, br, 4 * k:4 * k + 4]

            def cmp_chain(u):
                g, qt = divmod(u, 8)
                par = u % 2
                dd, rd, gr = smv(par, 0, 0), smv(par, 0, 1), smv(par, 0, 2)
                v6 = psb[6][:, 0:483].rearrange("p (h c) -> p h c", h=3)
                op("dve", lambda e: e.tensor_scalar(out=dd[:, 0:3].unsqueeze(2), in0=v6[:, :, 128:129], scalar1=1e-30,
                                                    scalar2=None, op0=ALU.max),
                   reads=[("ps", 6)], writes=[("dd", par, 0, 0)])
                op("dve", lambda e: e.tensor_scalar(out=dd[:, 3:4], in0=psb[7][:, 128:129], scalar1=1e-30,
                                                    scalar2=None, op0=ALU.max),
                   reads=[("ps", 7)], writes=[("dd", par, 0, 1)])
                op("dve", lambda e: e.reciprocal(out=rd, in_=dd), reads=[("dd", par, 0, 0), ("dd", par, 0, 1)],
                   writes=[("rd", par, 0)])
                op("dve", lambda e: e.tensor_tensor(out=gr, in0=rd, in1=gsig[:, qt, 4 * g:4 * g + 4], op=ALU.mult),
                   reads=[("rd", par, 0), ("gsig", qt)], writes=[("gr", par, 0)])
                for h in range(4):
                    b, ap, key = cps(h)
                    op("dve", lambda e, h=h, ap=ap: e.tensor_scalar(
                        out=ocomb[par][:, h * 128:(h + 1) * 128], in0=ap[:, 0:128], scalar1=gr[:, h:h + 1],
                        scalar2=None, op0=ALU.mult),
                       reads=[key, ("gr", par, 0)], writes=[("ocomb", par, h)])
                op("dve", lambda e: e.tensor_tensor(out=impw[:, 0:3, :], in0=v6[:, :, 129:161],
                                                    in1=rd[:, 0:3].unsqueeze(2).to_broadcast([128, 3, 32]), op=ALU.mult),
                   reads=[("ps", 6), ("rd", par, 0)], writes=["impw0"])
                op("dve", lambda e: e.tensor_scalar(out=impw[:, 3, :], in0=psb[7][:, 129:161], scalar1=rd[:, 3:4],
                                                    scalar2=None, op0=ALU.mult),
                   reads=[("ps", 7), ("rd", par, 0)], writes=["impw1"])
                op("dve", lambda e: e.tensor_reduce(out=imp[:, 0, :], in_=impw[:].rearrange("p h j -> p j h"),
                                                    axis=AX.X, op=ALU.add),
                   reads=["impw0", "impw1"], writes=["imp0"])
                op("dve", lambda e: e.tensor_tensor(out=imp[:, 1, :], in0=imp[:, 0, :], in1=F1e4[:, qt, :], op=ALU.max),
                   reads=["imp0", "cC"], writes=["imp1"])
                op("dve", lambda e: e.tensor_tensor(out=imp[:, 1, :], in0=imp[:, 1, :], in1=cvb[:, qt, :], op=ALU.mult),
                   reads=["imp1", "cC"], writes=["imp1"])
                op("dve", lambda e: e.tensor_tensor(out=imp[:, 1, :], in0=imp[:, 1, :], in1=cvbm1[:, qt, :], op=ALU.add),
                   reads=["imp1", "cC"], writes=["imp1"])
                op("dve", lambda e: e.max(out=top8[:], in_=imp[:, 1, :]), reads=["imp1"], writes=["top8"])
                op("dve", lambda e: e.tensor_scalar(out=imp[:, 2, :], in0=imp[:, 1, :], scalar1=top8[:, 7:8],
                                                    scalar2=None, op0=ALU.is_ge),
                   reads=["imp1", "top8"], writes=["imp2"])
                op("dve", lambda e: e.tensor_tensor(out=imp[:, 2, :], in0=imp[:, 2, :], in1=cvb[:, qt, :], op=ALU.mult),
                   reads=["imp2", "cC"], writes=["imp2"])
                op("dve", lambda e: e.tensor_scalar(out=selb[:, 0:32], in0=imp[:, 2, :], scalar1=-NEG, scalar2=NEG,
                                                    op0=ALU.mult, op1=ALU.add),
                   reads=["imp2"], writes=["selb"])

            def selb_transpose(u):
                par = u % 2
                mm(psb[7][:, 256:384], selb[:], ident, True, True, ["selb", "cB"], [("ps", 7)])
                op("act", lambda e: e.activation(out=selbT[:, par, :], in_=psb[7][:, 256:384], func=AF.Identity),
                   reads=[("ps", 7)], writes=[("selbT", par)])

            def branch_items(u, br):
                g, qt = divmod(u, 8)
                par = u % 2
                ql = 8 + qt
                kts = list(range(0, ql + 1)) if br == 1 else list(range(ql - 4, ql + 1))
                ob = (2, 3) if br == 1 else (4, 5)
                items = []
                for ki, kt in enumerate(kts):
                    def S(b, g=g, qt=qt, kt=kt, br=br, ql=ql, par=par):
                        ps = psb[b][:, 0:512].rearrange("p (h q) -> p h q", h=4)
                        rq = qT[:, 4 * g:4 * g + 4, qt * 128:(qt + 1) * 128]
                        qk = [("qT", 4 * g + h) for h in range(4)]
                        if br == 1:
                            diag = kt == ql
                            mm(ps, kTb[:, g, kt * 128:(kt + 1) * 128], rq, True, False, [("kTb", g)] + qk, [("ps", b)])
                            if not diag:
                                mm(ps, Emat[:, kt, :], selbT[:, par, :].unsqueeze(1).to_broadcast([128, 4, 128]),
                                   False, True, ["cB", ("selbT", par)], [("ps", b)])
                            if diag:
                                mm(ps, ident, tri_lo.unsqueeze(1).to_broadcast([128, 4, 128]), False, True,
                                   ["cB"], [("ps", b)])
                        else:
                            lo = kt == ql
                            hi = kt == ql - 4
                            mm(ps, kwT[:, g, (kt - 4) * 128:(kt - 3) * 128], rq, True, not (lo or hi),
                               [("kwT", g)] + qk, [("ps", b)])
                            if lo or hi:
                                mm(ps, ident, (tri_lo if lo else tri_hi).unsqueeze(1).to_broadcast([128, 4, 128]),
                                   False, True, ["cB"], [("ps", b)])

                    def E(b, p, kt=kt):
                        if kt >= 8:
                            op("act", lambda e: e.activation(out=pT[p][:], in_=psb[b][:, 0:512], func=AF.Exp,
                                                             bias=0.0, scale=SCALE),
                               reads=[("ps", b)], writes=[("pT", p)])
                            return
                        op("act", lambda e: e.activation(out=pT[p][:], in_=psb[b][:, 0:512], func=AF.Exp,
                                                         bias=cA[:, CA_KV + kt:CA_KV + kt + 1], scale=SCALE),
                           reads=[("ps", b), "cA"], writes=[("pT", p)])

                    def PV(p, g=g, kt=kt, br=br, ki=ki, n=len(kts)):
                        for h in range(4):
                            bb = ob[h // 2]
                            rhs = vsA[:, kt, g, :] if br == 1 else vwA[:, kt - 4, g, :]
                            rk = [("vs", kt, g // 2), "vs1"] if br == 1 else [("vw", kt, g // 2), "vw1"]
                            mm(psb[bb][:, (h % 2) * 129:(h % 2) * 129 + 129], pT[p][:, h * 128:(h + 1) * 128], rhs,
                               (ki == 0 and h % 2 == 0), ki == n - 1, [("pT", p)] + rk, [("ps", bb)])
                    items.append((S, E, PV))

                def fin(u=u, br=br, ob=ob, g=g, qt=qt, par=par):
                    dd, rd, gr = smv(par, br, 0), smv(par, br, 1), smv(par, br, 2)
                    for bi in range(2):
                        v = psb[ob[bi]][:, 0:258].rearrange("p (h c) -> p h c", h=2)
                        op("dve", lambda e, v=v, bi=bi: e.tensor_scalar(
                            out=dd[:, 2 * bi:2 * bi + 2].unsqueeze(2), in0=v[:, :, 128:129], scalar1=1e-30,
                            scalar2=None, op0=ALU.max),
                           reads=[("ps", ob[bi])], writes=[("dd", par, br, bi)])
                    op("dve", lambda e: e.reciprocal(out=rd, in_=dd), reads=[("dd", par, br, 0), ("dd", par, br, 1)],
                       writes=[("rd", par, br)])
                    op("dve", lambda e: e.tensor_tensor(out=gr, in0=rd, in1=gsig[:, qt, br * 16 + 4 * g:br * 16 + 4 * g + 4],
                                                        op=ALU.mult),
                       reads=[("rd", par, br), ("gsig", qt)], writes=[("gr", par, br)])
                    for h in range(4):
                        src = psb[ob[h // 2]][:, (h % 2) * 129:(h % 2) * 129 + 128]
                        if br == 1:
                            dst = ocomb[par][:, h * 128:(h + 1) * 128]
                            wk_ = [("ocomb", par, h)]
                        else:
                            dst = obf[:, h * 128:(h + 1) * 128]
                            wk_ = [("obf", h)]
                        op("dve", lambda e, h=h, src=src, dst=dst: e.scalar_tensor_tensor(
                            out=dst, in0=src, scalar=gr[:, h:h + 1], in1=ocomb[par][:, h * 128:(h + 1) * 128],
                            op0=ALU.mult, op1=ALU.add),
                           reads=[("ps", ob[h // 2]), ("gr", par, br), ("ocomb", par, h)], writes=wk_)
                return items, fin

            def o_transpose(u):
                g, qt = divmod(u, 8)
                b = sbank[0] % 2
                sbank[0] += 1
                for h in range(4):
                    mm(psb[b][:, h * 128:(h + 1) * 128], obf[:, h * 128:(h + 1) * 128], ident, True, True,
                       [("obf", h), "cB"], [("ps", b)])
                op("act", lambda e: e.activation(out=oT[:, 4 * g:4 * g + 4, qt * 128:(qt + 1) * 128],
                                                 in_=psb[b][:, 0:512].rearrange("p (h q) -> p h q", h=4), func=AF.Identity),
                   reads=[("ps", b)], writes=[("oT", 4 * g + h, qt) for h in range(4)] + [("hn", qt)])

            if mode == "pro":
                cmp_scores(0)
                cmp_pv(0)
                cmp_chain(0)
                return
            selb_transpose(0)
            G = []
            for u in range(32):
                its = []
                for br in (1, 2):
                    items, fin = branch_items(u, br)
                    for k, it in enumerate(items):
                        its.append([it[0], it[1], it[2], [fin] if k == len(items) - 1 else []])
                pre = []
                if u + 1 < 32:
                    if (u + 1) % 8 == 0:
                        pre.append(lambda u=u: cmp_scores((u + 1) // 8))
                    pre.append(lambda u=u: cmp_pv(u + 1))
                    pre.append(lambda u=u: cmp_chain(u + 1))
                its[0].append(pre)
                if u >= 1:
                    its[3][3].append(lambda u=u: o_transpose(u - 1))
                if u + 1 < 32:
                    its[min(8, len(its) - 3)][3].append(lambda u=u: selb_transpose(u + 1))
                G.extend(its)
            G[-1][3].append(lambda: o_transpose(31))
            slots = []
            n = len(G)

            def issue(j):
                slots.append((sbank[0] % 2, pbuf[0] % 3))
                sbank[0] += 1
                pbuf[0] += 1
                G[j][0](slots[j][0])
                G[j][1](slots[j][0], slots[j][1])
            issue(0)
            issue(1)
            for i in range(n):
                if len(G[i]) > 4:
                    for f in G[i][4]:
                        f()
                if i + 2 < n:
                    issue(i + 2)
                G[i][2](slots[i][1])
                for f in G[i][3]:
                    f()

        jobs.append((None, 0, lambda _: attention("main")))
        jobs.append((None, 0, lambda s: pg.barrier()))

        def za_job(pi):
            def fn(slot):
                for ct in range(2):
                    hd = pi * 2 + ct

                    def evac(ci, b, ps, t0, n, hd=hd):
                        zi = (hd * 2 + ci) % 2
                        op("act", lambda e: e.activation(out=pT[zi][:], in_=ps, func=AF.Silu, bias=bias_col("za", hd)),
                           reads=[("ps", b), "cA"], writes=[("zt", zi)])
                        op("dve", lambda e: e.tensor_tensor(out=oT[:, hd, t0 - 1024:t0 - 1024 + n],
                                                            in0=oT[:, hd, t0 - 1024:t0 - 1024 + n], in1=pT[zi][:], op=ALU.mult),
                           reads=[("zt", zi)] + [("oT", hd, q) for q in range(8)], writes=[("oaT", hd, ci)])
                    fm_tile(slot, ct * 128, OWNCH, evac)
            jobs.append((w_in[:, OFF["za"] + pi * 256:OFF["za"] + pi * 256 + 256], 256, fn))
        for pi in range(8):
            za_job(pi)

        ub = [wk[:, 0:1026], wk[:, 1026:2052]]
        yb = [wk[:, 2052:3076], wk[:, 3076:4100]]
        zs = [wk[:, 4100:4612], wk[:, 4612:5124]]

        def conv_jobs(c2):
            CH3 = [(1024, 342), (1366, 342), (1708, 342)]
            hk = [("hn", t) for t in range(8, 16)] + ["halo"]

            def fn_u(slot):
                for ct in range(2):
                    c = c2 * 2 + ct

                    def evac(ci, b, ps, t0, n, c=c, ct=ct):
                        o = t0 - 1024
                        no = min(n, 1024 - o)
                        op("act", lambda e: e.activation(out=ub[ct][:, 2 + o:2 + o + no], in_=ps[:, 0:no],
                                                         func=AF.Identity, bias=bias_col("u", c)),
                           reads=[("ps", b), "cA"], writes=[("ub", ct, ci)])
                        if no < n:
                            op("act", lambda e: e.activation(out=ub[ct][:, 0:2], in_=ps[:, no:n], func=AF.Identity,
                                                             bias=bias_col("u", c)),
                               reads=[("ps", b), "cA"], writes=[("ub", ct, 3)])
                    fm_tile(slot, ct * 128, CH3, evac, src=hnT, src_keys=hk)

            def fn_cc(slot):
                for ct in range(2):
                    c = c2 * 2 + ct

                    def evac(ci, b, ps, t0, n, c=c, ct=ct):
                        o = t0 - 1024
                        no = min(n, 1024 - o)
                        sl = ub[ct][:, 2 + o:2 + o + no]
                        op("dve", lambda e: e.scalar_tensor_tensor(out=sl, in0=ps[:, 0:no], scalar=bias_col("cc", c),
                                                                   in1=sl, op0=ALU.add, op1=ALU.mult),
                           reads=[("ps", b), "cA", ("ub", ct, ci)], writes=[("ub", ct, ci)])
                        if no < n:
                            sh = ub[ct][:, 0:2]
                            op("dve", lambda e: e.scalar_tensor_tensor(out=sh, in0=ps[:, no:n], scalar=bias_col("cc", c),
                                                                       in1=sh, op0=ALU.add, op1=ALU.mult),
                               reads=[("ps", b), "cA", ("ub", ct, 3)], writes=[("ub", ct, 3)])
                            op("dve", lambda e: e.tensor_scalar(out=sh, in0=sh, scalar1=cA[:, CA_HV:CA_HV + 1],
                                                                scalar2=None, op0=ALU.mult),
                               reads=[("ub", ct, 3), "cA"], writes=[("ub", ct, 3)])
                    fm_tile(slot, ct * 128, CH3, evac, src=hnT, src_keys=hk)
                    vk = [("ub", ct, 0), ("ub", ct, 1), ("ub", ct, 2), ("ub", ct, 3)]
                    cw = lambda k, c=c: cA[:, CA_CW + 3 * c + k:CA_CW + 3 * c + k + 1]
                    op("dve", lambda e, ct=ct, cw=cw: e.tensor_scalar(out=yb[ct], in0=ub[ct][:, 0:1024], scalar1=cw(0),
                                                                     scalar2=None, op0=ALU.mult),
                       reads=vk + ["cA"], writes=[("yb", ct)])
                    for k in (1, 2):
                        op("dve", lambda e, ct=ct, cw=cw, k=k: e.scalar_tensor_tensor(
                            out=yb[ct], in0=ub[ct][:, k:k + 1024], scalar=cw(k), in1=yb[ct], op0=ALU.mult, op1=ALU.add),
                           reads=vk + ["cA", ("yb", ct)], writes=[("yb", ct)])
                    op("dve", lambda e, ct=ct, c=c: e.tensor_scalar(out=yb[ct], in0=yb[ct],
                                                                   scalar1=cA[:, CA_CB + c:CA_CB + c + 1], scalar2=None,
                                                                   op0=ALU.add),
                       reads=[("yb", ct), "cA"], writes=[("yb", ct)])

            def fn_cb(slot):
                for ct in range(2):
                    c = c2 * 2 + ct

                    def evac(ci, b, ps, t0, n, c=c, ct=ct):
                        sl = yb[ct][:, t0 - 1024:t0 - 1024 + n]
                        op("dve", lambda e: e.scalar_tensor_tensor(out=sl, in0=ps, scalar=bias_col("cb", c), in1=sl,
                                                                   op0=ALU.add, op1=ALU.mult),
                           reads=[("ps", b), "cA", ("yb", ct)], writes=[("yb", ct)])
                    fm_tile(slot, ct * 128, OWNCH, evac)

            def fn_zc(slot):
                for ct in range(2):
                    c = c2 * 2 + ct

                    def evac(ci, b, ps, t0, n, c=c, ct=ct):
                        zi = ci
                        op("act", lambda e: e.activation(out=zs[zi], in_=ps, func=AF.Silu, bias=bias_col("zc", c)),
                           reads=[("ps", b), "cA"], writes=[("zs", zi)])
                        op("dve", lambda e: e.tensor_tensor(out=ycT[:, c, t0 - 1024:t0 - 1024 + n],
                                                            in0=yb[ct][:, t0 - 1024:t0 - 1024 + n], in1=zs[zi], op=ALU.mult),
                           reads=[("zs", zi), ("yb", ct)], writes=[("ycT", c)])
                    fm_tile(slot, ct * 128, OWNCH, evac)
            for seg, fn in (("u", fn_u), ("cc", fn_cc), ("cb", fn_cb), ("zc", fn_zc)):
                jobs.append((w_in[:, OFF[seg] + c2 * 256:OFF[seg] + c2 * 256 + 256], 256, fn))
        for c2 in range(8):
            conv_jobs(c2)
        jobs.append((None, 0, lambda s: pg.barrier()))

        g0 = wk[:, 0:2048].rearrange("p (t c n) -> p t c n", t=2, c=2)
        g1 = wk[:, 2048:4096].rearrange("p (t c n) -> p t c n", t=2, c=2)

        def merge_jobs(j2):
            def gate_fn(seg, gbuf):
                def fn(slot):
                    for ct in range(2):
                        j = j2 * 2 + ct

                        def evac(ci, b, ps, t0, n, j=j, ct=ct):
                            op("act", lambda e: e.activation(out=gbuf[:, ct, ci, :], in_=ps, func=AF.Sigmoid,
                                                             bias=bias_col(seg, j)),
                               reads=[("ps", b), "cA"], writes=[(seg, ct, ci)])
                        fm_tile(slot, ct * 128, OWNCH, evac)
                return fn

            def proj_fn(seg, gbuf, src, srck, last):
                def fn(slot):
                    for ct in range(2):
                        j = j2 * 2 + ct

                        def evac(ci, b, ps, t0, n, j=j, ct=ct):
                            op("dve", lambda e: e.tensor_tensor(out=gbuf[:, ct, ci, :], in0=ps, in1=gbuf[:, ct, ci, :],
                                                                op=ALU.mult),
                               reads=[("ps", b), (seg, ct, ci)], writes=[(seg, ct, ci)])
                            if last:
                                op("dve", lambda e: e.tensor_tensor(out=yT[:, j, t0:t0 + n], in0=g0[:, ct, ci, :],
                                                                    in1=g1[:, ct, ci, :], op=ALU.add),
                                   reads=[("gm0", ct, ci), ("gm1", ct, ci)], writes=[("yT", j)])
                        fm_tile(slot, ct * 128, [(0, 512), (512, 512)], evac, src=src, src_keys=srck)
                return fn
            oak = [("oaT", hd, ci) for hd in range(16) for ci in range(2)]
            yck = [("ycT", c) for c in range(16)]
            cs = slice(j2 * 256, j2 * 256 + 256)
            jobs.append((w_in[:, OFF["gm0"] + j2 * 256:OFF["gm0"] + j2 * 256 + 256], 256, gate_fn("gm0", g0)))
            jobs.append((w_in[:, OFF["gm1"] + j2 * 256:OFF["gm1"] + j2 * 256 + 256], 256, gate_fn("gm1", g1)))
            jobs.append((p_attn[:, cs], 256, proj_fn("gm0", g0, oT, oak, False)))
            jobs.append((p_conv[:, cs], 256, proj_fn("gm1", g1, ycT, yck, True)))
        for j2 in range(8):
            merge_jobs(j2)
        jobs.append((None, 0, lambda s: pg.barrier()))

        def load_resid(_):
            xo = xctx[1024:2048, :].rearrange("(q p) n -> p q n", p=128)
            load_xblock(0)
        jobs.append((None, 0, load_resid))

        def load_xblock(pj, gate=()):
            xo = xctx[1024:2048, :].rearrange("(q p) n -> p q n", p=128)
            op("sp", lambda e, pj=pj: e.dma_start(out=resid[:, :, pj * 256:(pj + 1) * 256],
                                                  in_=xo[:, :, pj * 256:(pj + 1) * 256]),
               reads=list(gate), writes=[("res", qt, pj) for qt in range(8)], dma_slot=("res", pj))

        def final_a(qt):
            rk = [("res", qt, pj) for pj in range(8)]
            op("act", lambda e, qt=qt: e.activation(out=wk[:, 0:2048], in_=resid[:, qt, :], func=AF.Square,
                                                    accum_out=stat[:, 48 + qt:49 + qt]),
               reads=rk, writes=["sqf", ("fss", qt)])
            op("act", lambda e, qt=qt: e.activation(out=stat[:, 56 + qt:57 + qt], in_=stat[:, 48 + qt:49 + qt],
                                                    func=AF.Sqrt, scale=1.0 / D, bias=EPS),
               reads=[("fss", qt)], writes=[("frms", qt)])

        def final_b(qt):
            rk = [("res", qt, pj) for pj in range(8)]
            op("dve", lambda e, qt=qt: e.reciprocal(out=stat[:, 40 + qt:41 + qt], in_=stat[:, 56 + qt:57 + qt]),
               reads=[("frms", qt)], writes=[("frstd", qt)])
            op("dve", lambda e, qt=qt: e.scalar_tensor_tensor(out=resid[:, qt, :], in0=resid[:, qt, :],
                                                              scalar=stat[:, 40 + qt:41 + qt], in1=fgb,
                                                              op0=ALU.mult, op1=ALU.mult),
               reads=rk + [("frstd", qt), "fg"], writes=rk)
            op("sp", lambda e, qt=qt: e.dma_start(out=outd[qt * 128:(qt + 1) * 128, :], in_=resid[:, qt, :]),
               reads=rk, dma_slot=("out", qt))

        def wo_job(pj):
            def fn(slot):
                g_ = [("pan", 1 - slot)]
                if pj + 1 < 8:
                    load_xblock(pj + 1, gate=g_)
                if pj == 3:
                    op("sp", lambda e: e.dma_start(out=fgb, in_=fgd), reads=g_, writes=["fg"], dma_slot="fg")
                for qt in range(8):
                    b = next_bank()
                    ps = psb[b][:, 0:256]
                    for kc in range(16):
                        mm(ps, yT[:, kc, qt * 128:(qt + 1) * 128], pan[slot][:, kc, 0:256], kc == 0, kc == 15,
                           [("pan", slot), ("yT", kc)], [("ps", b)])
                    sl = resid[:, qt, pj * 256:(pj + 1) * 256]
                    op("dve", lambda e, sl=sl, ps=ps: e.tensor_tensor(out=sl, in0=ps, in1=sl, op=ALU.add),
                       reads=[("ps", b), ("res", qt, pj)], writes=[("res", qt, pj)])
                    if pj == 7:
                        final_a(qt)
                        if qt >= 1:
                            final_b(qt - 1)
                        if qt == 7:
                            final_b(7)
            jobs.append((w_o[:, pj * 256:(pj + 1) * 256], 256, fn))
        for pj in range(8):
            wo_job(pj)

        fws = [("out", qt) for qt in range(8)]
        if dbg is not None:
            del jobs[dbg["stop"]:]
            views = dict(hnT=hnT[:], qT=qT, kTb=kTb, kwT=kwT, vsA=vsA, vwA=vwA, kcbT=kcbT[:], vcb=vcb[:], gsig=gsig[:],
                         oT=oT, ycT=ycT, yT=yT, resid=resid, cB=cB[:], hid=hid[:], expT=expT[:], selbT=selbT[:],
                         imp=imp[:], stat=stat[:])
            fws = []

            def dump(_):
                pg.barrier()
                for name in dbg["dump"]:
                    v = views[name]
                    dd = nc.dram_tensor("dbg_" + name, list(v.shape), v.dtype, kind="ExternalOutput").ap()
                    op("sp", lambda e, v=v, dd=dd: e.dma_start(out=dd, in_=v), dma_slot=("dbg", name))
                    fws.append(("dbg", name))
            jobs.append((None, 0, dump))
        run_jobs()
        pg.emit(final_wait_slots=fws)
    return nc


def _host_consts(h):
    i = np.arange(OWN)
    t_glob = i + (OWN if h == 1 else 0)
    j = np.arange(32)
    jg = j - (0 if h == 1 else 16)
    valid = jg >= 0
    causal = (jg[None, :] * 64 <= t_glob[:, None]) & valid[None, :]
    cur = t_glob // 64
    forced = ((jg[None, :] == 0) | (jg[None, :] == cur[:, None]) | (jg[None, :] == cur[:, None] - 1)) & causal
    to_t = lambda a: np.ascontiguousarray(a.reshape(8, 128, 32).transpose(1, 0, 2).reshape(128, 256))
    cC = np.concatenate([to_t(forced.astype(np.float32) * 8192.0), to_t(causal.astype(np.float32)),
                         to_t(causal.astype(np.float32) - 1.0)], axis=1).astype(np.float32)
    kvalid = np.zeros((128, 16), np.float32)
    cvalid = np.zeros((128, 1), np.float32)
    if h == 0:
        kvalid[:, 0:8] = NEG
        cvalid[0:64] = NEG
    hv = np.full((128, 1), float(h), np.float32)
    p = np.arange(128)
    ident = np.eye(128, dtype=np.float32)
    tri_lo = np.where(p[:, None] <= p[None, :], 0.0, NEG).astype(np.float32)
    tri_hi = np.where(p[:, None] > p[None, :], 0.0, NEG).astype(np.float32)
    E = np.zeros((128, 16, 128), np.float32)
    for kt in range(16):
        for k in range(128):
            E[2 * kt + k // 64, kt, k] = 1.0
    n = np.arange(128)
    cmpb = np.where(16 * n[:, None] + 31 <= 1024 + i[None, :], 0.0, NEG).astype(np.float32)
    c0 = (np.arange(128) * 16)[:, None]
    s0 = (np.arange(32) * 64)[None, :]
    agg = (np.maximum(0, np.minimum(c0 + 32, s0 + 64) - np.maximum(c0, s0)) / 32.0).astype(np.float32)
    cB = np.concatenate([ident, tri_lo, tri_hi, E.reshape(128, 2048), cmpb, agg], axis=1).astype(np.float32)
    return cC, kvalid, cvalid, hv, cB


_NC_CACHE = {}


def kernel(x, norm_g, w_in, b_in, cmp_pe_k, cmp_w1_k, cmp_w2_k, cmp_pe_v, cmp_w1_v, cmp_w2_v,
           conv_w, conv_b, p_attn, p_conv, w_o, final_g):
    f = lambda a: np.ascontiguousarray(np.asarray(a, dtype=np.float32))
    x = f(x)
    w_in0 = f(w_in)[0]
    b = f(b_in)[0]
    ng = f(norm_g)[0]
    cwv = f(conv_w)[0]
    cbv = f(conv_b)[0]
    bfm = np.zeros((128, NFM), np.float32)
    for seg, cnt in FMSEG:
        for t in range(cnt):
            bfm[:, FMB[seg] + t] = b[OFF[seg] + t * 128:OFF[seg] + (t + 1) * 128]
    ngc = ng.reshape(16, 128).T
    cwc = cwv.reshape(3, 16, 128).transpose(2, 1, 0).reshape(128, 48)
    cbc = cbv.reshape(16, 128).T
    pek = f(cmp_pe_k)[0].T
    pev = f(cmp_pe_v)[0].T
    bgn = np.broadcast_to(b[OFF["gn"]:OFF["gn"] + 48], (128, 48))
    w2 = np.concatenate([f(cmp_w2_k)[0].reshape(2, 128, 128).transpose(1, 0, 2).reshape(128, 256),
                         f(cmp_w2_v)[0].reshape(2, 128, 128).transpose(1, 0, 2).reshape(128, 256)], axis=1)
    bv = np.concatenate([np.broadcast_to(b[OFF["vs"]:OFF["vs"] + 512], (128, 512)),
                         np.broadcast_to(b[OFF["vw"]:OFF["vw"] + 512], (128, 512))], axis=1)
    fg = np.broadcast_to(f(final_g), (128, D))
    shared = dict(w_in=w_in0, p_attn=f(p_attn)[0], p_conv=f(p_conv)[0], w_o=f(w_o)[0],
                  w1k=f(cmp_w1_k)[0], w1v=f(cmp_w1_v)[0], w2=np.ascontiguousarray(w2),
                  bv=np.ascontiguousarray(bv), fg=np.ascontiguousarray(fg))
    hc = [_host_consts(0), _host_consts(1)]
    in_maps = []
    for c in range(8):
        bi, h = divmod(c, 2)
        cC, kvalid, cvalid, hv, cB = hc[h]
        cA = np.concatenate([bfm, ngc, cwc, cbc, kvalid, cvalid, hv, pek, pev, bgn], axis=1).astype(np.float32)
        assert cA.shape == (128, CA_N)
        if h == 1:
            xc = x[bi]
        else:
            xc = np.concatenate([np.zeros((OWN, D), np.float32), x[bi, 0:OWN]], axis=0)
        m = dict(shared)
        m.update(xctx=np.ascontiguousarray(xc), cA=np.ascontiguousarray(cA), cB=cB, cC=cC)
        in_maps.append(m)
    if "nc" not in _NC_CACHE:
        _NC_CACHE["nc"] = build_program()
    res = run_bass_kernel_spmd(_NC_CACHE["nc"], in_maps, core_ids=list(range(8)))
    out = np.zeros((4, S, D), np.float32)
    for c in range(8):
        bi, h = divmod(c, 2)
        out[bi, h * OWN:(h + 1) * OWN] = res.results[c]["out"]
    return out
```
